# Optimizing a Trainium2 kernel written in Bass

```python
import jax, jax.numpy as jnp
from jax import lax
import numpy as np

D_MODEL = 1024
BATCH = 8
SEQ = 4096
DEPTH = 2

CHUNK = 64
SB_HEADS = 8
SB_HEAD_DIM = 64
SB_WIDTH = SB_HEADS * SB_HEAD_DIM
SB_BLOCK = 128
SG_GROUPS = 4
SG_GROUP_DIM = 128
SG_WIDTH = SG_GROUPS * SG_GROUP_DIM
SG_CHUNK = 128
POOL_WINDOWS = (2, 4, 8, 16)
POOL_GROUPS = len(POOL_WINDOWS)
POOL_GROUP_DIM = 128
POOL_WIDTH = POOL_GROUPS * POOL_GROUP_DIM
N_BRANCHES = 3
BRANCH_WIDTH = 512
D_FF = 2816
CONV_WIDTH = 3
EPS = 1e-6
IN_SIZES = (SB_WIDTH, SB_WIDTH, SB_WIDTH, SG_WIDTH, SG_WIDTH, POOL_WIDTH, N_BRANCHES * D_MODEL)
IN_WIDTH = sum(IN_SIZES)
IN_SPLITS = tuple(int(i) for i in np.cumsum(IN_SIZES)[:-1])

kernel_name = "hybrid_stickbreak_sgu_pool_block"


def rmsnorm(x, g):
    xf = x.astype(jnp.float32)
    y = xf * lax.rsqrt(jnp.mean(xf * xf, axis=-1, keepdims=True) + EPS)
    return (y * g.astype(jnp.float32)).astype(x.dtype)


def layernorm(x, g, b):
    xf = x.astype(jnp.float32)
    mu = jnp.mean(xf, axis=-1, keepdims=True)
    var = jnp.mean(jnp.square(xf - mu), axis=-1, keepdims=True)
    y = (xf - mu) * lax.rsqrt(var + EPS)
    return (y * g.astype(jnp.float32) + b.astype(jnp.float32)).astype(x.dtype)


def stick_breaking_attention(q, k, v):
    _, _, S, Dh = q.shape
    scale = Dh ** -0.5
    outs = []
    for blk in range(S // SB_BLOCK):
        q0 = blk * SB_BLOCK
        end = q0 + SB_BLOCK
        qb = q[:, :, q0:end].astype(jnp.float32)
        kb = k[:, :, :end].astype(jnp.float32)
        vb = v[:, :, :end]
        z = jnp.einsum('bhtd,bhsd->bhts', qb, kb) * scale
        t_idx = q0 + jnp.arange(SB_BLOCK)[:, None]
        s_idx = jnp.arange(end)[None, :]
        strict = s_idx < t_idx
        log_keep = jnp.where(strict, jax.nn.log_sigmoid(-z), 0.0)
        after = lax.cumsum(log_keep, axis=3, reverse=True) - log_keep
        w = jnp.where(strict, jnp.exp(jax.nn.log_sigmoid(z) + after), 0.0)
        outs.append(jnp.einsum('bhts,bhsd->bhtd', w.astype(vb.dtype), vb))
    return jnp.concatenate(outs, axis=2)


def spatial_gating(u, v, w_s, b_s, ln_g, ln_b):
    B, S, _ = u.shape
    u = jax.nn.gelu(u)
    v = layernorm(jax.nn.gelu(v), ln_g, ln_b)
    vc = v.reshape(B, S // SG_CHUNK, SG_CHUNK, SG_GROUPS, SG_GROUP_DIM)
    causal = jnp.tril(jnp.ones((SG_CHUNK, SG_CHUNK), dtype=bool))
    w = jnp.where(causal[None], w_s, 0.0).astype(vc.dtype)
    mixed = jnp.einsum('gts,bnsgc->bntgc', w, vc) + b_s.T[None, None, :, :, None]
    return u * mixed.reshape(B, S, SG_WIDTH)


def multiscale_pool(xp, w_pool, scale):
    B, S, _ = xp.shape
    xg = xp.astype(jnp.float32).reshape(B, S, POOL_GROUPS, POOL_GROUP_DIM)
    cs = jnp.cumsum(xg, axis=1)
    n_valid = jnp.arange(1, S + 1)
    outs = []
    for g, win in enumerate(POOL_WINDOWS):
        c = cs[:, :, g]
        lagged = jnp.pad(c, ((0, 0), (win, 0), (0, 0)))[:, :S]
        count = jnp.minimum(n_valid, win).astype(jnp.float32)[None, :, None]
        pooled = (c - lagged) / count - xg[:, :, g]
        outs.append(pooled @ w_pool[g].astype(jnp.float32))
    y = jnp.concatenate(outs, axis=-1) * scale.astype(jnp.float32)
    return y.astype(xp.dtype)


def causal_depthwise_conv(h, w, b):
    S = h.shape[1]
    hp = jnp.pad(h, ((0, 0), (CONV_WIDTH - 1, 0), (0, 0)))
    y = b
    for i in range(CONV_WIDTH):
        y = y + hp[:, i:i + S] * w[i]
    return y


def setup_inputs(seed: int = 0) -> dict:
    key = jax.random.key(seed)
    ks = jax.random.split(key, 24)
    f32 = jnp.float32
    nrm = lambda k, shape, s: jax.random.normal(k, shape, f32) * s
    gain = lambda k, shape: 1.0 + 0.05 * jax.random.normal(k, shape, f32)
    L = DEPTH
    return {
        "x": jax.random.normal(ks[0], (BATCH, SEQ, D_MODEL), f32),
        "norm_pre_mix": gain(ks[1], (L, D_MODEL)),
        "w_in": nrm(ks[2], (L, D_MODEL, IN_WIDTH), D_MODEL ** -0.5),
        "sg_ln_g": gain(ks[3], (L, SG_WIDTH)),
        "sg_ln_b": nrm(ks[4], (L, SG_WIDTH), 0.02),
        "sg_w": nrm(ks[5], (L, SG_GROUPS, SG_CHUNK, SG_CHUNK), 0.5 * SG_CHUNK ** -0.5),
        "sg_b": gain(ks[6], (L, SG_GROUPS, SG_CHUNK)),
        "pool_w": nrm(ks[7], (L, POOL_GROUPS, POOL_GROUP_DIM, POOL_GROUP_DIM), POOL_GROUP_DIM ** -0.5),
        "pool_scale": gain(ks[8], (L, POOL_WIDTH)),
        "w_branch": nrm(ks[9], (L, N_BRANCHES, BRANCH_WIDTH, D_MODEL), BRANCH_WIDTH ** -0.5),
        "w_out": nrm(ks[10], (L, D_MODEL, D_MODEL), D_MODEL ** -0.5),
        "norm_post_mix": gain(ks[11], (L, D_MODEL)),
        "norm_pre_ffn": gain(ks[12], (L, D_MODEL)),
        "w_up": nrm(ks[13], (L, D_MODEL, 2 * D_FF), D_MODEL ** -0.5),
        "conv_w": nrm(ks[14], (L, CONV_WIDTH, 2 * D_FF), CONV_WIDTH ** -0.5),
        "conv_b": nrm(ks[15], (L, 2 * D_FF), 0.02),
        "w_down": nrm(ks[16], (L, D_FF, D_MODEL), D_FF ** -0.5),
        "norm_post_ffn": gain(ks[17], (L, D_MODEL)),
    }


def reference(x, norm_pre_mix, w_in, sg_ln_g, sg_ln_b, sg_w, sg_b, pool_w, pool_scale,
              w_branch, w_out, norm_post_mix, norm_pre_ffn, w_up, conv_w, conv_b, w_down,
              norm_post_ffn):
    B, S, D = x.shape
    to_heads = lambda t: t.reshape(B, S, SB_HEADS, SB_HEAD_DIM).transpose(0, 2, 1, 3)
    for l in range(DEPTH):
        h = rmsnorm(x, norm_pre_mix[l])
        proj = h @ w_in[l]
        q, k, v, u_sg, v_sg, x_pool, gate_logits = jnp.split(proj, IN_SPLITS, axis=-1)
        a = stick_breaking_attention(to_heads(q), to_heads(k), to_heads(v))
        a = a.transpose(0, 2, 1, 3).reshape(B, S, SB_WIDTH)
        b = spatial_gating(u_sg, v_sg, sg_w[l], sg_b[l], sg_ln_g[l], sg_ln_b[l])
        c = multiscale_pool(x_pool, pool_w[l], pool_scale[l])
        gates = jax.nn.sigmoid(gate_logits.reshape(B, S, N_BRANCHES, D))
        merged = (gates[:, :, 0] * (a @ w_branch[l, 0])
                  + gates[:, :, 1] * (b @ w_branch[l, 1])
                  + gates[:, :, 2] * (c @ w_branch[l, 2]))
        x = x + rmsnorm(merged @ w_out[l], norm_post_mix[l])
        h = rmsnorm(x, norm_pre_ffn[l])
        up = causal_depthwise_conv(h @ w_up[l], conv_w[l], conv_b[l])
        gt, val = jnp.split(up, 2, axis=-1)
        f = (jax.nn.gelu(gt, approximate=True) * val) @ w_down[l]
        x = x + rmsnorm(f, norm_post_ffn[l])
    return x
```

```python
import numpy as np
from contextlib import ExitStack
import concourse.bass as bass
import concourse.mybir as mybir
from concourse.bass_utils import run_bass_kernel_spmd

F32 = mybir.dt.float32
BF16 = mybir.dt.bfloat16
AF = mybir.ActivationFunctionType
ALU = mybir.AluOpType

D = 1024
S = 4096
TT = 512
NT = S // TT
NST = 4
KC = 8
DFF = 2816
NFC = DFF // 128
INW = 6144
EPS = 1e-6
WINS = (2, 4, 8, 16)
NCORES = 8


class Res:
    __slots__ = ("name", "w", "r")

    def __init__(self, name):
        self.name = name
        self.w = None
        self.r = {}


class Builder:
    def __init__(self, nc, es, L):
        self.nc = nc
        self.es = es
        self.L = L
        self.eng = {"pe": nc.tensor, "act": nc.scalar, "dve": nc.vector, "pool": nc.gpsimd, "sp": nc.sync}
        self.sems = {}
        self.seq = {"pe": 0, "act": 0, "dve": 0, "pool": 0}
        for k in self.seq:
            self.sems[k] = es.enter_context(nc.semaphore("s_" + k))
        self.dcount = {}
        self.waited = {e: {} for e in self.eng}
        self.ninstr = 0

    def dsem(self, key):
        if key not in self.sems:
            self.sems[key] = self.es.enter_context(self.nc.semaphore("d_" + key))
            self.dcount[key] = 0
        return self.sems[key]

    def _waits(self, eng, reads, writes):
        need = {}

        def add(k, v):
            if k in self.dcount:
                v = self.dcount[k]
            if need.get(k, 0) < v:
                need[k] = v

        for r in reads:
            if r.w is not None:
                add(*r.w)
        for w in writes:
            if w.w is not None and w.w[0] != eng:
                add(*w.w)
            for (k, e), v in w.r.items():
                if e != eng:
                    add(k, v)
        wt = self.waited[eng]
        for k, v in need.items():
            if wt.get(k, 0) < v:
                self.eng[eng].wait_ge(self.sems[k], v)
                wt[k] = v
                self.ninstr += 1

    def _register(self, ev, ekey, reads, writes):
        k, v = ev
        for r in reads:
            key = (k, ekey)
            if r.r.get(key, 0) < v:
                r.r[key] = v
        for w in writes:
            w.w = ev
            w.r = {}

    def op(self, eng, fn, reads=(), writes=(), inc=True):
        self._waits(eng, reads, writes)
        ins = fn()
        self.ninstr += 1
        ev = (eng, self.seq[eng] + 1)
        if inc:
            ins.then_inc(self.sems[eng], 1)
            self.seq[eng] += 1
        self._register(ev, eng, reads, writes)
        return ins

    def dma(self, q, out_ap, in_ap, reads, writes, semkey):
        sem = self.dsem(semkey)
        self._waits(q, reads, writes)
        ins = self.eng[q].dma_start(out=out_ap, in_=in_ap)
        self.dcount[semkey] += 16
        ins.then_inc(sem, 16)
        self.ninstr += 1
        ev = (semkey, self.dcount[semkey])
        self._register(ev, "dma", reads, writes)
        return ev

    def mm(self, out, lhsT, rhs, reads, writes, start=True, stop=True, inc=False, **kw):
        nc = self.nc
        return self.op("pe", lambda: nc.tensor.matmul(out, lhsT, rhs, start=start, stop=stop, **kw),
                       reads, writes, inc=inc)

    def act(self, out, in_, func, reads, writes, **kw):
        nc = self.nc
        return self.op("act", lambda: nc.scalar.activation(out=out, in_=in_, func=func, **kw), reads, writes)

    def tt(self, out, in0, in1, op, reads, writes, eng="dve"):
        e = self.eng[eng]
        return self.op(eng, lambda: e.tensor_tensor(out, in0, in1, op), reads, writes)

    def ts(self, out, in0, s1, s2, op0, op1, reads, writes, eng="dve"):
        e = self.eng[eng]
        if op1 is None:
            return self.op(eng, lambda: e.tensor_scalar(out, in0, s1, s2, op0), reads, writes)
        return self.op(eng, lambda: e.tensor_scalar(out, in0, s1, s2, op0, op1), reads, writes)

    def stt(self, out, in0, scalar, in1, op0, op1, reads, writes, eng="dve"):
        e = self.eng[eng]
        return self.op(eng, lambda: e.scalar_tensor_tensor(out, in0, scalar, in1, op0, op1), reads, writes)

    def cp(self, out, in_, reads, writes, eng="dve"):
        e = self.eng[eng]
        return self.op(eng, lambda: e.tensor_copy(out, in_), reads, writes)

    def memset(self, ap, val, writes, eng="dve"):
        e = self.eng[eng]
        return self.op(eng, lambda: e.memset(ap, val), (), writes)


class Ring:
    def __init__(self, items):
        self.items = list(items)
        self.i = 0

    def next(self):
        it = self.items[self.i % len(self.items)]
        self.i += 1
        return it


class StopBuild(Exception):
    pass


DBG = {"att_iters": None, "stage": None}


class Stream:
    def __init__(self, banks):
        self.banks = list(banks)
        self.ring = list(banks)
        self.ptr = 0

    def nbank(self, reserve=False):
        b = self.ring[self.ptr % len(self.ring)]
        if reserve:
            self.ring.remove(b)
        else:
            self.ptr += 1
        return b

    def release(self, b):
        self.ring.append(b)
        self.ring.sort()


def build_program(L):
    nc = bass.Bass("TRN2", target_bir_lowering=False)
    es = ExitStack()
    B = Builder(nc, es, L)
    E = es.enter_context

    def din(name, shape):
        return nc.dram_tensor(name, list(shape), F32, kind="ExternalInput").ap()

    x_d = din("x", [S, D])
    out_d = nc.dram_tensor("out", [S, D], F32, kind="ExternalOutput").ap()
    c_ident = din("c_ident", [128, 128])
    c_tri = din("c_tri", [128, 4, 128])
    c_ntri = din("c_ntri", [128, 2, 128])
    c_pool = din("c_pool", [128, 4, 3, 128])
    c_invcnt = din("c_invcnt", [1, 4 * 128])
    W = []
    for l in range(L):
        W.append(dict(
            w_in=din(f"w_in{l}", [D, INW]),
            w_branch=din(f"w_branch{l}", [3, 512, D]),
            w_out=din(f"w_out{l}", [D, D]),
            w_up=din(f"w_up{l}", [D, 2 * DFF]),
            w_down=din(f"w_down{l}", [DFF, D]),
            gcols=din(f"gcols{l}", [128, 2 * KC]),
            gpost=din(f"gpost{l}", [1, 2 * D]),
            lngb=din(f"lngb{l}", [1, 2 * 512]),
            sgwT=din(f"sgwT{l}", [128, 4, 128]),
            sgb=din(f"sgb{l}", [1, 512]),
            poolw=din(f"poolw{l}", [128, 4, 128]),
            pscale=din(f"pscale{l}", [128, 4]),
            convp=din(f"convp{l}", [128, 2 * NFC, 4]),
        ))

    def sb(name, shape, dt):
        return E(nc.sbuf_tensor("sb_" + name, list(shape), dt))

    NX = 3
    xts = [sb(f"xt{i}", [128, NST, D], F32) for i in range(NX)]
    xs_res = [[Res(f"x{i}_{st}") for st in range(NST)] for i in range(NX)]
    kscr = nc.dram_tensor("kscr", [L, NT, 4, 128, TT], BF16).ap()
    kscr_res = [[[Res(f"kscr{l}_{t}_{p}") for p in range(4)] for t in range(NT)] for l in range(L)]
    vscr = nc.dram_tensor("vscr", [L, NT, 4, 128, NST * 128], BF16).ap()
    vscr_res = [[[Res(f"vscr{l}_{t}_{p}") for p in range(4)] for t in range(NT)] for l in range(L)]
    Vtile = sb("Vtile", [128, NST, 512], BF16)
    Vt_res = [Res(f"Vt{st}") for st in range(NST)]
    NVB = 3
    vbuf = [sb(f"vbuf{i}", [128, NST, 128], BF16) for i in range(NVB)]
    vbuf_res = [Res(f"vbuf{i}") for i in range(NVB)]
    vrr = [0]
    kbuf = [sb(f"kbuf{i}", [128, TT], BF16) for i in range(NVB)]
    kbuf_res = [Res(f"kbuf{i}") for i in range(NVB)]
    krr = [0]

    identF = sb("identF", [128, 128], F32)
    tri = sb("tri", [128, 4, 128], BF16)
    ntri = sb("ntri", [128, 2, 128], BF16)
    leF = sb("leF", [128, 128], F32)
    poolM = sb("poolM", [128, 4, 3, 128], BF16)
    invcnt = sb("invcnt", [128, 4 * 128], F32)
    ones_row = sb("ones_row", [1, 128], BF16)
    zerosb = sb("zerosb", [128, 64], BF16)
    const_res = Res("consts")

    P_ = []
    for l in range(L):
        P_.append(dict(
            gcols=sb(f"gcols{l}", [128, 2 * KC], F32),
            sgwT=sb(f"sgwT{l}", [128, 4, 128], BF16),
            sgbhi=sb(f"sgbhi{l}", [1, 512], BF16),
            sgblo=sb(f"sgblo{l}", [1, 512], BF16),
            poolw=sb(f"poolw{l}", [128, 4, 128], BF16),
            pscale=sb(f"pscale{l}", [128, 4], F32),
            convp=sb(f"convp{l}", [128, 2 * NFC, 4], F32),
            halo=sb(f"halo{l}", [128, 2 * NFC, 2], F32),
            xp_prev=sb(f"xp_prev{l}", [128, 512], BF16),
            res=Res(f"params{l}"),
            halo_res=[Res(f"halo{l}_{c}") for c in range(2 * NFC)],
            xpp_res=Res(f"xpp{l}"),
        ))

    gpost_sb = sb("gpost_sb", [128, D], F32)
    gpost_res = Res("gpost")
    lngb_sb = sb("lngb_sb", [128, 2 * 512], F32)
    lngb_res = Res("lngb")
    NW = 3
    wbuf = [sb(f"wbuf{i}", [128, 4096], BF16) for i in range(NW)]
    wres = [Res(f"wbuf{i}") for i in range(NW)]
    wrr = [0]

    NS16 = 73
    NS32 = 10
    s16 = sb("s16", [128, NS16, 512], BF16)
    s32 = sb("s32", [128, NS32, 512], F32)
    s16_res = [Res(f"s16_{i}") for i in range(NS16)]
    s32_res = [Res(f"s32_{i}") for i in range(NS32)]
    free16 = list(range(NS16))
    free32 = list(range(NS32))
    peak = {"16": 0, "32": 0}

    class Slot:
        __slots__ = ("i", "ap", "res", "kind")

        def __init__(self, i, ap, res, kind):
            self.i, self.ap, self.res, self.kind = i, ap, res, kind

    def a16():
        assert free16, "out of bf16 slots"
        i = free16.pop(0)
        peak["16"] = max(peak["16"], NS16 - len(free16))
        return Slot(i, s16[:, i, :], s16_res[i], 16)

    def a32():
        assert free32, "out of fp32 slots"
        i = free32.pop(0)
        peak["32"] = max(peak["32"], NS32 - len(free32))
        return Slot(i, s32[:, i, :], s32_res[i], 32)

    def fr(*slots):
        for s_ in slots:
            (free16 if s_.kind == 16 else free32).append(s_.i)

    junk = sb("junk", [128, 1024], BF16)
    junk_res = Res("junk")
    stat = sb("stat", [128, 16, 8], F32)
    stat_res = [Res(f"stat{i}") for i in range(16)]
    stat_ring = Ring(range(16))

    ps = E(nc.psum_tensor("ps", [128, 8, 512], F32))
    bank_res = [Res(f"bank{i}") for i in range(8)]
    SM = Stream(range(0, 5))
    SF = Stream(range(5, 8))
    SM.dg = sb("dg_m", [128, NST, 128], F32)
    SM.dg_res = [Res(f"dgm{i}") for i in range(NST)]
    SF.dg, SF.dg_res = SM.dg, SM.dg_res

    def sdma(q, out_ap, in_ap):
        B.dma(q, out_ap, in_ap, [], [Res("setup")], "c_" + q)

    sdma("sp", identF[:], c_ident)
    le_res = Res("le")
    B.dma("sp", leF[:], c_tri[:, 3, :], [], [le_res], "cle")
    sdma("sp", invcnt[:], c_invcnt.partition_broadcast(128))
    sdma("pool", tri[:], c_tri)
    sdma("pool", ntri[:], c_ntri)
    sdma("pool", poolM[:], c_pool)
    B.memset(ones_row[:], 1.0, [Res("setup")])
    B.memset(zerosb[:], 0.0, [Res("setup")])
    for l in range(L):
        p = P_[l]
        w = W[l]
        sdma("sp", p["gcols"][:], w["gcols"])
        sdma("sp", p["pscale"][:], w["pscale"])
        sdma("sp", p["convp"][:], w["convp"])
        sdma("pool", p["poolw"][:], w["poolw"])
        tmp = a32()
        B.dma("sp", tmp.ap, w["sgwT"].rearrange("p g t -> p (g t)"), [], [tmp.res], f"ctmp{l}")
        for g in range(4):
            B.tt(p["sgwT"][:, g, :], tmp.ap[:, g * 128:(g + 1) * 128], leF[:], ALU.mult,
                 [tmp.res, le_res], [Res("setup")])
        fr(tmp)
        sgb_res = Res("sgb")
        t32a, t32b = a32(), a32()
        B.dma("sp", t32a.ap[0:1, :], w["sgb"], [], [sgb_res], f"csgb{l}")
        B.cp(p["sgbhi"][:], t32a.ap[0:1, :], [sgb_res], [sgb_res])
        B.cp(t32b.ap[0:1, :], p["sgbhi"][:], [sgb_res], [sgb_res])
        B.tt(p["sgblo"][:], t32a.ap[0:1, :], t32b.ap[0:1, :], ALU.subtract, [sgb_res], [sgb_res])
        fr(t32a, t32b)
        B.memset(p["halo"][:], 0.0, [Res("setup")])
        B.memset(p["xp_prev"][:], 0.0, [Res("setup")])

    for e in ("pe", "act", "dve", "pool", "sp"):
        for k in list(B.dcount):
            B.eng[e].wait_ge(B.sems[k], B.dcount[k])
            B.waited[e][k] = B.dcount[k]
        if B.seq["dve"] > 0:
            B.eng[e].wait_ge(B.sems["dve"], B.seq["dve"])
            B.waited[e]["dve"] = B.seq["dve"]
    for r_ in s32_res + s16_res:
        r_.w = None
        r_.r = {}

    def wload(src3, nk, ncols):
        i = wrr[0] % NW
        wrr[0] += 1
        dst = wbuf[i][:, 0:nk * ncols].rearrange("p (k c) -> p k c", k=nk)
        B.dma("pool", dst, src3, [], [wres[i]], f"w{i}")
        return dst, wres[i]

    def win_cols(l, c0, n):
        return W[l]["w_in"][:, c0:c0 + n].rearrange("(k p) c -> p k c", p=128)

    def rstd_chain(si, src_col, n):
        r = stat_res[si]
        B.act(stat[:, si, 6:7], stat[:, si, src_col:src_col + 1], AF.Ln, [r], [r], scale=1.0 / n, bias=EPS)
        B.act(stat[:, si, 7:8], stat[:, si, 6:7], AF.Exp, [r], [r], scale=-0.5)

    def prenorm_gen(l, which, xb, S_):
        p = P_[l]
        xt = xts[xb]
        x_res = xs_res[xb]
        hT = [a16() for _ in range(KC)]
        for st in range(NST):
            si = stat_ring.next()
            r = stat_res[si]
            B.memset(stat[:, si, 0:1], 0.0, [r])
            B.act(junk[:, :], xt[:, st, :], AF.Square,
                  [x_res[st], r], [junk_res, r], accum_out=stat[:, si, 0:1])
            rstd_chain(si, 0, D)
            B.ts(S_.dg[:, st, :], identF[:], stat[:, si, 7:8], None, ALU.mult, None,
                 [r], [S_.dg_res[st]])
        yield hT
        pend = None
        for kc in range(KC + 1):
            if kc < KC:
                b = S_.nbank()
                for st in range(NST):
                    B.mm(ps[:, b, st * 128:(st + 1) * 128], xt[:, st, kc * 128:(kc + 1) * 128], S_.dg[:, st, :],
                         [x_res[st], S_.dg_res[st]], [bank_res[b]], inc=(st == NST - 1))
            if pend is not None:
                pk, pb_ = pend
                B.act(hT[pk].ap, ps[:, pb_, :], AF.Identity, [bank_res[pb_]], [hT[pk].res],
                      scale=p["gcols"][:, which * KC + pk: which * KC + pk + 1])
            pend = (kc, b) if kc < KC else None
            yield hT

    def proj_fm(bank, wt, wr, k0, hT):
        for kc in range(KC):
            B.mm(ps[:, bank, :], wt[:, kc, k0:k0 + 128], hT[kc].ap, [wr, hT[kc].res], [bank_res[bank]],
                 start=(kc == 0), stop=(kc == KC - 1), inc=(kc == KC - 1))

    def proj_tm(bank, wt, wr, st, hT):
        for kc in range(KC):
            B.mm(ps[:, bank, :], hT[kc].ap[:, st * 128:(st + 1) * 128], wt[:, kc, :], [wr, hT[kc].res],
                 [bank_res[bank]], start=(kc == 0), stop=(kc == KC - 1), inc=(kc == KC - 1))

    def postnorm_residual(xb, st, A, bank):
        xt = xts[xb]
        xr = xs_res[xb][st]
        si = stat_ring.next()
        r = stat_res[si]
        br = bank_res[bank]
        B.memset(stat[:, si, 0:2], 0.0, [r])
        B.act(junk[:, 0:512], A.ap, AF.Square, [A.res, r], [junk_res, r], accum_out=stat[:, si, 0:1])
        B.act(junk[:, 512:1024], ps[:, bank, :], AF.Square, [br, r], [junk_res, r], accum_out=stat[:, si, 1:2])
        B.tt(stat[:, si, 2:3], stat[:, si, 0:1], stat[:, si, 1:2], ALU.add, [r], [r])
        rstd_chain(si, 2, D)
        pb = a32()
        B.stt(A.ap, A.ap, stat[:, si, 7:8], gpost_sb[:, 0:512],
              ALU.mult, ALU.mult, [A.res, r, gpost_res], [A.res])
        B.stt(pb.ap, ps[:, bank, :], stat[:, si, 7:8], gpost_sb[:, 512:1024],
              ALU.mult, ALU.mult, [br, r, gpost_res], [pb.res])
        B.tt(xt[:, st, 0:512], xt[:, st, 0:512], A.ap, ALU.add, [xr, A.res], [xr])
        B.tt(xt[:, st, 512:1024], xt[:, st, 512:1024], pb.ap, ALU.add, [xr, pb.res], [xr])
        fr(pb)

    def attention_gen(l, T, pr, qs, Kt, aT_slot, S_):
        accb = S_.nbank(reserve=True)
        ar = bank_res[accb]
        Eslots = [a32() for _ in range(2)]
        Pslots = [a16() for _ in range(4)]
        wslots = [a16() for _ in range(3)]
        psum_slots = [[a16(), a16()] for _ in range(2)]
        Ering, Pring, wring = Ring(Eslots), Ring(Pslots), Ring(wslots)
        for h in range(2):
            B.mm(ps[h * 64:(h + 1) * 64, accb, :], zerosb[:, :], qs[0].ap,
                 [qs[0].res], [ar], start=True, stop=False, inc=(h == 1), tile_position=(0, h * 64),
                 skip_group_check=True)
        for hf in range(2):
            B.memset(psum_slots[hf][0].ap, 0.0, [psum_slots[hf][0].res])
        blocks = list(range(T * 4 + 3, -1, -1))
        items = []
        for j in blocks:
            jl = j - T * 4
            c0 = jl * 128 if jl >= 0 else 0
            for hf in range(2):
                if c0 >= (hf + 1) * 256:
                    continue
                lc0 = max(0, c0 - hf * 256)
                diag = (jl >= 0 and c0 // 256 == hf)
                items.append((j, hf, lc0, diag))
        firstj = {1: T * 4 + 3, 0: T * 4 + 1}
        n = len(items)
        st_ = {}
        pidx = [0, 0]
        vchunk = {}
        kchunk = {}

        def ensure_v(tp):
            if tp >= 0 and tp not in vchunk:
                k = vrr[0] % NVB
                vrr[0] += 1
                B.dma("sp", vbuf[k][:], vscr[l, tp, pr].rearrange("s (b c) -> s b c", b=NST),
                      [vscr_res[l][tp][pr]], [vbuf_res[k]], f"vld{k}")
                vchunk[tp] = k

        def ensure_k(tp):
            if tp >= 0 and tp not in kchunk:
                k = krr[0] % NVB
                krr[0] += 1
                B.dma("sp", kbuf[k][:], kscr[l, tp, pr], [kscr_res[l][tp][pr]], [kbuf_res[k]], f"kld{k}")
                kchunk[tp] = k

        def vblock(j, h):
            jl = j - T * 4
            if jl >= 0:
                return Vtile[:, jl, (2 * pr + h) * 64:(2 * pr + h + 1) * 64], Vt_res[jl]
            tp = j // 4
            ensure_v(tp)
            ensure_v(tp - 1)
            k = vchunk[tp]
            return vbuf[k][:, j % 4, h * 64:(h + 1) * 64], vbuf_res[k]

        def kblock(j, rows):
            jl = j - T * 4
            if jl >= 0:
                return Kt.ap[rows, jl * 128:(jl + 1) * 128], Kt.res
            tp = j // 4
            ensure_k(tp)
            ensure_k(tp - 1)
            k = kchunk[tp]
            return kbuf[k][rows, (j % 4) * 128:(j % 4 + 1) * 128], kbuf_res[k]

        ensure_k(T - 1)
        ensure_v(T - 1)

        def view(ap2d, lc0):
            if lc0 == 0:
                return ap2d
            return ap2d.rearrange("p (h c) -> p h c", h=2)[:, :, lc0:]

        def stageA(i):
            j, hf, lc0, diag = items[i]
            zb = S_.nbank()
            kap, kres = kblock(j, slice(0, 128))
            for h in range(2):
                B.mm(ps[:, zb, h * 256 + lc0:(h + 1) * 256], kap, qs[h].ap[:, hf * 256 + lc0:(hf + 1) * 256],
                     [kres, qs[h].res], [bank_res[zb]], start=(h == 0), stop=(h == 1), inc=(h == 1))
            if DBG["stage"] == -1:
                raise StopBuild()
            e_ = Ering.next()
            B.act(view(e_.ap, lc0), view(ps[:, zb, :], lc0), AF.Exp, [bank_res[zb]], [e_.res])
            if DBG["stage"] == -2:
                raise StopBuild()
            P = Pring.next()
            B.act(view(P.ap, lc0), view(e_.ap, lc0), AF.Ln, [e_.res], [P.res], bias=1.0)
            if DBG["stage"] == -3:
                raise StopBuild()
            if diag:
                for h in range(2):
                    cs = slice(h * 256 + lc0, h * 256 + lc0 + 128)
                    B.tt(P.ap[:, cs], P.ap[:, cs], tri[:, 2, :], ALU.mult, [P.res], [P.res])
            st_[i] = dict(P=P, zb=zb)

        def stageB(i):
            j, hf, lc0, diag = items[i]
            d = st_[i]
            P, b2 = d["P"], d["zb"]
            first = (j == firstj[hf])
            last = (j == 0)
            cur = psum_slots[hf][pidx[hf] % 2]
            if lc0 == 0:
                segs = [slice(0, 512)]
            else:
                segs = [slice(h * 256 + lc0, (h + 1) * 256) for h in range(2)]
            for si_, sg_ in enumerate(segs):
                lastseg = (si_ == len(segs) - 1)
                B.mm(ps[:, b2, sg_], ntri[:, 0, :], P.ap[:, sg_], [P.res], [bank_res[b2]],
                     start=False, stop=first, inc=(first and lastseg), skip_group_check=True)
                if not first:
                    B.mm(ps[:, b2, sg_], ntri[:, 1, :], cur.ap[:, sg_], [cur.res], [bank_res[b2]],
                         start=False, stop=True, inc=lastseg, skip_group_check=True)
            w_ = wring.next()
            B.act(view(w_.ap, lc0), view(ps[:, b2, :], lc0), AF.Exp, [bank_res[b2]], [w_.res])
            if diag:
                for h in range(2):
                    cs = slice(h * 256 + lc0, h * 256 + lc0 + 128)
                    B.tt(w_.ap[:, cs], w_.ap[:, cs], tri[:, 2, :], ALU.mult, [w_.res], [w_.res])
            if not last:
                nxt = psum_slots[hf][(pidx[hf] + 1) % 2]
                if lc0 > 0:
                    v0 = lambda ap: ap.rearrange("p (h c) -> p h c", h=2)[:, :, 0:lc0]
                    B.cp(v0(nxt.ap), v0(cur.ap), [cur.res], [nxt.res])
                B.tt(view(nxt.ap, lc0), view(cur.ap, lc0), view(P.ap, lc0), ALU.add, [cur.res, P.res], [nxt.res])
                pidx[hf] += 1
            d["w"] = w_

        def stageC(i):
            j, hf, lc0, diag = items[i]
            d = st_.pop(i)
            w_ = d["w"]
            for h in range(2):
                vap, vres = vblock(j, h)
                B.mm(ps[h * 64:(h + 1) * 64, accb, hf * 256 + lc0:(hf + 1) * 256], vap,
                     w_.ap[:, h * 256 + lc0:(h + 1) * 256], [vres, w_.res], [ar], start=False, stop=(j == 0),
                     inc=(h == 1), tile_position=(0, h * 64), skip_group_check=True)

        SK = 2
        if DBG["att_iters"] is not None:
            ii_ = DBG["att_iters"]
            stageA(ii_)
            if DBG["stage"] >= 1:
                stageB(ii_)
            if DBG["stage"] >= 2:
                stageC(ii_)
            raise StopBuild()
        for i in range(n + 2 * SK):
            if i < n:
                stageA(i)
            if 0 <= i - SK < n:
                stageB(i - SK)
            if 0 <= i - 2 * SK < n:
                stageC(i - 2 * SK)
            if i < n + 2 * SK - 1:
                yield "att"
        B.act(aT_slot.ap, ps[:, accb, :], AF.Copy, [ar], [aT_slot.res])
        S_.release(accb)
        fr(*Eslots, *Pslots, *wslots, *psum_slots[0], *psum_slots[1])

    def n_att_steps(T):
        return 4 * (8 * T + 6 + 4)

    wstash = {}

    def prefetch_merge(l, T):
        if ("wb", l, T, 0) in wstash:
            return
        wstash[("wb", l, T, 0)] = wload(W[l]["w_branch"][0].rearrange("(r p) c -> p r c", p=128), 4, D)
        wstash[("wg", l, T, 0, 0)] = wload(win_cols(l, 3072, 512), KC, 512)

    def mixer_gen(l, T):
        S_ = SM
        p = P_[l]
        xb = T % NX
        xt = xts[xb]
        if l == 0:
            xin = x_d[T * TT:(T + 1) * TT, :].rearrange("(s p) d -> p s d", p=128)
            B.dma("sp", xt[:], xin, [], xs_res[xb], "xin")
        hT = None
        for hT in prenorm_gen(l, 0, xb, S_):
            yield "pre"
        wt, wr = wload(win_cols(l, 1024, 512), KC, 512)
        for st in range(NST):
            b = S_.nbank()
            proj_tm(b, wt, wr, st, hT)
            B.act(Vtile[:, st, :], ps[:, b, :], AF.Copy, [bank_res[b]], [Vt_res[st]])
            yield "pre"
        if T < NT - 1:
            for pr in range(4):
                B.dma("sp", vscr[l, T, pr].rearrange("s (b c) -> s b c", b=NST), Vtile[:, :, pr * 128:(pr + 1) * 128],
                      Vt_res, [vscr_res[l][T][pr]], f"vst{pr}")
        B.dma("sp", lngb_sb[:], W[l]["lngb"].partition_broadcast(128), [], [lngb_res], "lngb")
        wt, wr = wload(win_cols(l, 2048, 512), KC, 512)
        vsg = [a16() for _ in range(NST)]
        for st in range(NST):
            b = S_.nbank()
            proj_tm(b, wt, wr, st, hT)
            si = stat_ring.next()
            r = stat_res[si]
            gl = a32()
            B.memset(stat[:, si, 0:2], 0.0, [r])
            B.act(gl.ap, ps[:, b, :], AF.Gelu_apprx_tanh, [bank_res[b], r], [gl.res, r], accum_out=stat[:, si, 0:1])
            B.act(junk[:, 0:512], gl.ap, AF.Square, [gl.res, r], [junk_res, r], accum_out=stat[:, si, 1:2])
            B.ts(stat[:, si, 3:4], stat[:, si, 0:1], 1.0 / 512, None, ALU.mult, None, [r], [r])
            B.tt(stat[:, si, 4:5], stat[:, si, 3:4], stat[:, si, 3:4], ALU.mult, [r], [r])
            B.stt(stat[:, si, 5:6], stat[:, si, 1:2], 1.0 / 512, stat[:, si, 4:5], ALU.mult, ALU.subtract, [r], [r])
            B.act(stat[:, si, 6:7], stat[:, si, 5:6], AF.Ln, [r], [r], bias=EPS)
            B.act(stat[:, si, 7:8], stat[:, si, 6:7], AF.Exp, [r], [r], scale=-0.5)
            B.ts(gl.ap, gl.ap, stat[:, si, 3:4], stat[:, si, 7:8], ALU.subtract, ALU.mult, [gl.res, r], [gl.res])
            B.tt(gl.ap, gl.ap, lngb_sb[:, 0:512], ALU.mult, [gl.res, lngb_res], [gl.res])
            B.tt(vsg[st].ap, gl.ap, lngb_sb[:, 512:1024], ALU.add, [gl.res, lngb_res], [vsg[st].res])
            fr(gl)
            yield "pre"
        wt, wr = wload(win_cols(l, 2560, 512), KC, 512)
        xp = [a16() for _ in range(NST)]
        for st in range(NST):
            b = S_.nbank()
            proj_tm(b, wt, wr, st, hT)
            B.act(xp[st].ap, ps[:, b, :], AF.Copy, [bank_res[b]], [xp[st].res])
            yield "pre"
        wt, wr = wload(win_cols(l, 1536, 512), KC, 512)
        bT = [a16() for _ in range(4)]
        for g in range(4):
            b = S_.nbank()
            proj_fm(b, wt, wr, g * 128, hT)
            u = a16()
            B.act(u.ap, ps[:, b, :], AF.Gelu_apprx_tanh, [bank_res[b]], [u.res])
            b2 = S_.nbank()
            for st in range(NST):
                cs = slice(st * 128, (st + 1) * 128)
                B.mm(ps[:, b2, cs], vsg[st].ap[:, g * 128:(g + 1) * 128], p["sgwT"][:, g, :],
                     [vsg[st].res], [bank_res[b2]], start=True, stop=False)
                B.mm(ps[:, b2, cs], ones_row[0:1, :], p["sgbhi"][0:1, g * 128:(g + 1) * 128],
                     [], [bank_res[b2]], start=False, stop=False)
                B.mm(ps[:, b2, cs], ones_row[0:1, :], p["sgblo"][0:1, g * 128:(g + 1) * 128],
                     [], [bank_res[b2]], start=False, stop=True, inc=(st == NST - 1))
            B.tt(bT[g].ap, ps[:, b2, :], u.ap, ALU.mult, [bank_res[b2], u.res], [bT[g].res])
            fr(u)
            yield "pre"
        fr(*vsg)
        cT = [a16() for _ in range(4)]
        for g in range(4):
            b = S_.nbank()
            for st in range(NST):
                cs = slice(st * 128, (st + 1) * 128)
                gs = slice(g * 128, (g + 1) * 128)
                if T == 0 and st == 0:
                    B.mm(ps[:, b, cs], xp[0].ap[:, gs], poolM[:, g, 2, :], [xp[0].res], [bank_res[b]],
                         start=True, stop=True)
                else:
                    B.mm(ps[:, b, cs], xp[st].ap[:, gs], poolM[:, g, 0, :], [xp[st].res],
                         [bank_res[b]], start=True, stop=False)
                    if st == 0:
                        pv, pvr = p["xp_prev"][:, gs], p["xpp_res"]
                    else:
                        pv, pvr = xp[st - 1].ap[:, gs], xp[st - 1].res
                    B.mm(ps[:, b, cs], pv, poolM[:, g, 1, :], [pvr], [bank_res[b]],
                         start=False, stop=True, inc=(st == NST - 1))
            pT = a16()
            if T == 0:
                B.tt(pT.ap[:, 0:128], ps[:, b, 0:128], invcnt[:, g * 128:(g + 1) * 128], ALU.mult,
                     [bank_res[b]], [pT.res])
                B.act(pT.ap[:, 128:512], ps[:, b, 128:512], AF.Identity, [bank_res[b]], [pT.res], scale=1.0 / WINS[g])
            else:
                B.act(pT.ap, ps[:, b, :], AF.Identity, [bank_res[b]], [pT.res], scale=1.0 / WINS[g])
            b2 = S_.nbank()
            B.mm(ps[:, b2, :], p["poolw"][:, g, :], pT.ap, [pT.res], [bank_res[b2]], inc=True)
            B.act(cT[g].ap, ps[:, b2, :], AF.Identity, [bank_res[b2]], [cT[g].res], scale=p["pscale"][:, g:g + 1])
            fr(pT)
            yield "pre"
        B.cp(p["xp_prev"][:, :], xp[3].ap, [xp[3].res], [p["xpp_res"]])
        fr(*xp)
        wq, wqr = wload(win_cols(l, 0, 512), KC, 512)
        wk, wkr = wload(win_cols(l, 512, 512), KC, 512)
        qs = [(a16(), a16()) for _ in range(4)]
        Kt = [a16() for _ in range(4)]
        for pr in range(4):
            bq = S_.nbank()
            proj_fm(bq, wq, wqr, pr * 128, hT)
            bk = S_.nbank()
            proj_fm(bk, wk, wkr, pr * 128, hT)
            for h in range(2):
                rows, orow = slice(h * 64, (h + 1) * 64), slice((1 - h) * 64, (2 - h) * 64)
                B.act(qs[pr][h].ap[rows, :], ps[rows, bq, :], AF.Identity, [bank_res[bq]], [qs[pr][h].res], scale=0.125)
                B.memset(qs[pr][h].ap[orow, :], 0.0, [qs[pr][h].res])
            B.cp(Kt[pr].ap, ps[:, bk, :], [bank_res[bk]], [Kt[pr].res])
            if T < NT - 1:
                B.dma("sp", kscr[l, T, pr], Kt[pr].ap, [Kt[pr].res], [kscr_res[l][T][pr]], f"kst{pr}")
            yield "pre"
        aT = [a16() for _ in range(4)]
        for pr in range(4):
            yield from attention_gen(l, T, pr, qs[pr], Kt[pr], aT[pr], S_)
            fr(*qs[pr], Kt[pr])
        yield "attdone"
        macc = [a32() for _ in range(KC)]
        merged = [a16() for _ in range(KC)]
        srcs = [aT, bT, cT]
        for i in range(3):
            wb, wbr = wstash.pop(("wb", l, T, i), None) or \
                wload(W[l]["w_branch"][i].rearrange("(r p) c -> p r c", p=128), 4, D)
            for half in range(2):
                wg, wgr = wstash.pop(("wg", l, T, i, half), None) or \
                    wload(win_cols(l, 3072 + i * D + half * 512, 512), KC, 512)
                for dc in range(4):
                    d = half * 4 + dc
                    by = S_.nbank()
                    for r_ in range(4):
                        B.mm(ps[:, by, :], wb[:, r_, d * 128:(d + 1) * 128], srcs[i][r_].ap, [wbr, srcs[i][r_].res],
                             [bank_res[by]], start=(r_ == 0), stop=(r_ == 3), inc=(r_ == 3))
                    bg = S_.nbank()
                    proj_fm(bg, wg, wgr, dc * 128, hT)
                    sg = a32()
                    B.act(sg.ap, ps[:, bg, :], AF.Sigmoid, [bank_res[bg]], [sg.res])
                    if i == 0:
                        B.tt(macc[d].ap, ps[:, by, :], sg.ap, ALU.mult, [bank_res[by], sg.res], [macc[d].res])
                    else:
                        B.tt(sg.ap, ps[:, by, :], sg.ap, ALU.mult, [bank_res[by], sg.res], [sg.res])
                        if i == 1:
                            B.tt(macc[d].ap, macc[d].ap, sg.ap, ALU.add, [macc[d].res, sg.res], [macc[d].res])
                        else:
                            B.tt(merged[d].ap, macc[d].ap, sg.ap, ALU.add, [macc[d].res, sg.res], [merged[d].res])
                    fr(sg)
                    yield "post"
        fr(*aT, *bT, *cT, *hT, *macc)
        A = [a32() for _ in range(NST)]
        for half in range(2):
            wo, wor = wload(W[l]["w_out"][:, half * 512:(half + 1) * 512].rearrange("(k p) c -> p k c", p=128), KC, 512)
            for st in range(NST):
                b = S_.banks[st]
                for kc in range(KC):
                    B.mm(ps[:, b, :], merged[kc].ap[:, st * 128:(st + 1) * 128], wo[:, kc, :],
                         [merged[kc].res, wor], [bank_res[b]], start=(kc == 0), stop=(kc == KC - 1),
                         inc=(kc == KC - 1))
                if half == 0:
                    B.act(A[st].ap, ps[:, b, :], AF.Copy, [bank_res[b]], [A[st].res])
                yield "post"
        B.dma("sp", gpost_sb[:], W[l]["gpost"][:, 0:D].partition_broadcast(128), [], [gpost_res], "gpost")
        for st in range(NST):
            postnorm_residual(xb, st, A[st], S_.banks[st])
            yield "post"
        fr(*merged, *A)

    def ffn_gen(l, T):
        S_ = SF
        p = P_[l]
        xb = T % NX
        hT = None
        for hT in prenorm_gen(l, 1, xb, S_):
            yield "ffn"
        gsl = [a16() for _ in range(NFC)]
        cw = p["convp"]
        NH = 2 * NFC
        hb = {}
        t1s = {}
        wts = {}

        def ch_of(h):
            i = h // 2
            return i if h % 2 == 0 else NFC + i

        def get_w(h):
            i = h // 2
            g0 = (i // 4) * 4
            key = (g0, h % 2)
            if key not in wts:
                ng = min(4, NFC - g0)
                c0 = g0 * 128 + (DFF if h % 2 else 0)
                wts[key] = wload(W[l]["w_up"][:, c0:c0 + ng * 128].rearrange("(k p) c -> p k c", p=128), KC, ng * 128)
            wt, wr = wts[key]
            return wt, wr, (i - g0) * 128

        for step_ in range(NH + 6):
            h = step_
            if h < NH:
                wt, wr, k0 = get_w(h)
                b = S_.nbank()
                proj_fm(b, wt, wr, k0, hT)
                hb[h] = b
            h = step_ - 1
            if 0 <= h < NH:
                b = hb[h]
                ch = ch_of(h)
                t1 = a32()
                B.act(t1.ap, ps[:, b, :], AF.Identity, [bank_res[b]], [t1.res],
                      scale=cw[:, ch, 2:3], bias=cw[:, ch, 3:4])
                t1s[h] = t1
            h = step_ - 2
            if 0 <= h < NH:
                b = hb.pop(h)
                ch = ch_of(h)
                hr = p["halo_res"][ch]
                t1 = t1s[h]
                B.stt(t1.ap[:, 1:512], ps[:, b, 0:511], cw[:, ch, 1:2], t1.ap[:, 1:512], ALU.mult, ALU.add,
                      [bank_res[b], t1.res], [t1.res])
                B.stt(t1.ap[:, 2:512], ps[:, b, 0:510], cw[:, ch, 0:1], t1.ap[:, 2:512], ALU.mult, ALU.add,
                      [bank_res[b], t1.res], [t1.res])
                if T > 0:
                    B.stt(t1.ap[:, 0:1], p["halo"][:, ch, 1:2], cw[:, ch, 1:2], t1.ap[:, 0:1], ALU.mult, ALU.add,
                          [hr, t1.res], [t1.res])
                    B.stt(t1.ap[:, 0:2], p["halo"][:, ch, 0:2], cw[:, ch, 0:1], t1.ap[:, 0:2], ALU.mult, ALU.add,
                          [hr, t1.res], [t1.res])
                B.cp(p["halo"][:, ch, :], ps[:, b, 510:512], [bank_res[b]], [hr])
            h = step_ - 4
            if 0 <= h < NH and h % 2 == 1 and (h // 2) % 2 == 1:
                for i_ in (h // 2 - 1, h // 2):
                    tg = t1s[2 * i_]
                    B.act(tg.ap, tg.ap, AF.Gelu_apprx_tanh, [tg.res], [tg.res])
            h = step_ - 5
            if 0 <= h < NH and h % 2 == 1 and (h // 2) % 2 == 1:
                for i_ in (h // 2 - 1, h // 2):
                    tg, tv = t1s.pop(2 * i_), t1s.pop(2 * i_ + 1)
                    B.tt(gsl[i_].ap, tg.ap, tv.ap, ALU.mult, [tg.res, tv.res], [gsl[i_].res])
                    fr(tg, tv)
            yield "ffn"
        fr(*hT)
        for sp_ in range(2):
            A = {}
            for half in range(2):
                bks = {2 * sp_: S_.banks[0], 2 * sp_ + 1: S_.banks[1]}
                i0 = 0
                while i0 < NFC:
                    ng = min(8, NFC - i0)
                    wd, wdr = wload(W[l]["w_down"][i0 * 128:(i0 + ng) * 128, half * 512:(half + 1) * 512]
                                    .rearrange("(r p) c -> p r c", p=128), ng, 512)
                    for ii in range(ng):
                        i = i0 + ii
                        for st in (2 * sp_, 2 * sp_ + 1):
                            b = bks[st]
                            lastmm = (st == 2 * sp_ + 1 and (ii % 4 == 3 or ii == ng - 1))
                            B.mm(ps[:, b, :], gsl[i].ap[:, st * 128:(st + 1) * 128], wd[:, ii, :],
                                 [gsl[i].res, wdr], [bank_res[b]], start=(i == 0), stop=(i == NFC - 1), inc=lastmm)
                        if ii % 4 == 3 or ii == ng - 1:
                            yield "ffn"
                    i0 += ng
                if half == 0:
                    for st in (2 * sp_, 2 * sp_ + 1):
                        A[st] = a32()
                        B.cp(A[st].ap, ps[:, bks[st], :], [bank_res[bks[st]]], [A[st].res])
                    yield "ffn"
            if sp_ == 0:
                B.dma("sp", gpost_sb[:], W[l]["gpost"][:, D:2 * D].partition_broadcast(128), [], [gpost_res], "gpost")
            yield "ffn"
            for st in (2 * sp_, 2 * sp_ + 1):
                postnorm_residual(xb, st, A[st], bks[st])
                yield "ffn"
            fr(*A.values())
        fr(*gsl)
        if l == L - 1:
            xout = out_d[T * TT:(T + 1) * TT, :].rearrange("(s p) d -> p s d", p=128)
            B.dma("sp", xout, xts[xb][:], xs_res[xb], [], "xout")

    units = sorted([(T, l) for T in range(NT) for l in range(L)], key=lambda u: (u[0] + 2 * u[1], -u[1]))
    N_FFN_STEPS = 75

    def step(g):
        try:
            return next(g)
        except StopIteration:
            return None

    pending_ffn = None
    for (T, l) in units:
        mg = mixer_gen(l, T)
        tag = step(mg)
        while tag == "pre":
            tag = step(mg)
        natt = n_att_steps(T)
        every = max(1, natt // N_FFN_STEPS) if pending_ffn is not None else 0
        cnt = 0
        while tag == "att":
            cnt += 1
            if pending_ffn is not None and cnt % every == 0:
                if step(pending_ffn) is None:
                    pending_ffn = None
                    prefetch_merge(l, T)
            tag = step(mg)
        while pending_ffn is not None:
            if step(pending_ffn) is None:
                pending_ffn = None
        while tag is not None:
            tag = step(mg)
        pending_ffn = ffn_gen(l, T)
    while pending_ffn is not None:
        if step(pending_ffn) is None:
            pending_ffn = None

    nc.sync.wait_ge(B.sems["xout"], B.dcount["xout"])
    es.close()
    B.peak = peak
    return nc, B


def _consts():
    ident = np.eye(128, dtype=np.float32)
    s = np.arange(128)[:, None]
    t = np.arange(128)[None, :]
    tri = np.zeros((128, 4, 128), np.float32)
    tri[:, 0, :] = (s >= t)
    tri[:, 1, :] = 1.0
    tri[:, 2, :] = (s < t)
    tri[:, 3, :] = (s <= t)
    pool = np.zeros((128, 4, 3, 128), np.float32)
    invc = np.zeros((4, 128), np.float32)
    for g, win in enumerate(WINS):
        cur = ((s <= t) & (s >= t - win + 1)).astype(np.float32) - win * (s == t)
        prev = (s >= 129 + t - win).astype(np.float32)
        cnt = np.minimum(np.arange(128) + 1, win)
        first = ((s <= t) & (s >= t - win + 1)).astype(np.float32) - cnt[None, :] * (s == t)
        pool[:, g, 0, :] = cur
        pool[:, g, 1, :] = prev
        pool[:, g, 2, :] = first
        invc[g] = 1.0 / cnt
    return ident, tri, pool, invc.reshape(1, 512)


def _layer_inputs(inp, l, k):
    f = lambda a: np.ascontiguousarray(a, dtype=np.float32)
    cols = lambda v: v.reshape(-1, 128).T
    gcols = np.concatenate([cols(inp["norm_pre_mix"][l]), cols(inp["norm_pre_ffn"][l])], axis=1)
    gpost = np.concatenate([inp["norm_post_mix"][l], inp["norm_post_ffn"][l]])[None, :]
    lngb = np.concatenate([inp["sg_ln_g"][l], inp["sg_ln_b"][l]])[None, :]
    convp = np.concatenate([inp["conv_w"][l], inp["conv_b"][l][None, :]], axis=0)
    convp = convp.reshape(4, 2 * NFC, 128).transpose(2, 1, 0)
    return {
        f"w_in{k}": f(inp["w_in"][l]), f"w_branch{k}": f(inp["w_branch"][l]), f"w_out{k}": f(inp["w_out"][l]),
        f"w_up{k}": f(inp["w_up"][l]), f"w_down{k}": f(inp["w_down"][l]),
        f"gcols{k}": f(gcols), f"gpost{k}": f(gpost), f"lngb{k}": f(lngb),
        f"sgwT{k}": f(inp["sg_w"][l].transpose(2, 0, 1)), f"sgb{k}": f(inp["sg_b"][l].reshape(1, 512)),
        f"poolw{k}": f(inp["pool_w"][l].transpose(1, 0, 2)), f"pscale{k}": f(cols(inp["pool_scale"][l])),
        f"convp{k}": f(convp),
    }


_PROG = {}


def _get_prog(L):
    if L not in _PROG:
        _PROG[L] = build_program(L)[0]
    return _PROG[L]


FUSED_LAYERS = 2


def kernel(**inputs):
    inp = {k: np.asarray(v) for k, v in inputs.items()}
    depth = inp["w_in"].shape[0]
    ident, tri, pool, invc = _consts()
    x = np.ascontiguousarray(inp["x"], dtype=np.float32)
    Lp = FUSED_LAYERS
    nc = _get_prog(Lp)
    for l0 in range(0, depth, Lp):
        shared = {"c_ident": ident, "c_tri": tri, "c_pool": pool, "c_invcnt": invc,
                  "c_ntri": np.ascontiguousarray(-tri[:, 0:2, :])}
        for k in range(Lp):
            shared.update(_layer_inputs(inp, l0 + k, k))
        in_maps = [dict(shared, x=x[c]) for c in range(NCORES)]
        res = run_bass_kernel_spmd(nc, in_maps, core_ids=list(range(NCORES)))
        x = np.stack([np.asarray(res.results[c]["out"]) for c in range(NCORES)], axis=0).astype(np.float32)
    return x
```

```python
import numpy as np
from contextlib import ExitStack
import concourse.bass as bass
import concourse.mybir as mybir
from concourse.bass_utils import run_bass_kernel_spmd

F32 = mybir.dt.float32
BF16 = mybir.dt.bfloat16
AF = mybir.ActivationFunctionType
ALU = mybir.AluOpType

D = 1024
S = 4096
TT = 512
NT = S // TT
NST = 4
KC = 8
DFF = 2816
NFC = DFF // 128
INW = 6144
EPS = 1e-6
WINS = (2, 4, 8, 16)
NCORES = 8


class Res:
    __slots__ = ("name", "w", "r")

    def __init__(self, name):
        self.name = name
        self.w = None
        self.r = {}


class Builder:
    def __init__(self, nc, es, L):
        self.nc = nc
        self.es = es
        self.L = L
        self.eng = {"pe": nc.tensor, "act": nc.scalar, "dve": nc.vector, "pool": nc.gpsimd, "sp": nc.sync}
        self.sems = {}
        self.seq = {"pe": 0, "act": 0, "dve": 0, "pool": 0}
        for k in self.seq:
            self.sems[k] = es.enter_context(nc.semaphore("s_" + k))
        self.dcount = {}
        self.waited = {e: {} for e in self.eng}
        self.ninstr = 0

    def dsem(self, key):
        if key not in self.sems:
            self.sems[key] = self.es.enter_context(self.nc.semaphore("d_" + key))
            self.dcount[key] = 0
        return self.sems[key]

    def _waits(self, eng, reads, writes):
        need = {}

        def add(k, v):
            if k in self.dcount:
                v = self.dcount[k]
            if need.get(k, 0) < v:
                need[k] = v

        for r in reads:
            if r.w is not None:
                add(*r.w)
        for w in writes:
            if w.w is not None and w.w[0] != eng:
                add(*w.w)
            for (k, e), v in w.r.items():
                if e != eng:
                    add(k, v)
        wt = self.waited[eng]
        for k, v in need.items():
            if wt.get(k, 0) < v:
                self.eng[eng].wait_ge(self.sems[k], v)
                wt[k] = v
                self.ninstr += 1

    def _register(self, ev, ekey, reads, writes):
        k, v = ev
        for r in reads:
            key = (k, ekey)
            if r.r.get(key, 0) < v:
                r.r[key] = v
        for w in writes:
            w.w = ev
            w.r = {}

    def op(self, eng, fn, reads=(), writes=(), inc=True):
        self._waits(eng, reads, writes)
        ins = fn()
        self.ninstr += 1
        ev = (eng, self.seq[eng] + 1)
        if inc:
            ins.then_inc(self.sems[eng], 1)
            self.seq[eng] += 1
        self._register(ev, eng, reads, writes)
        return ins

    def dma(self, q, out_ap, in_ap, reads, writes, semkey):
        sem = self.dsem(semkey)
        self._waits(q, reads, writes)
        ins = self.eng[q].dma_start(out=out_ap, in_=in_ap)
        self.dcount[semkey] += 16
        ins.then_inc(sem, 16)
        self.ninstr += 1
        ev = (semkey, self.dcount[semkey])
        self._register(ev, "dma", reads, writes)
        return ev

    def mm(self, out, lhsT, rhs, reads, writes, start=True, stop=True, inc=False, **kw):
        nc = self.nc
        return self.op("pe", lambda: nc.tensor.matmul(out, lhsT, rhs, start=start, stop=stop, **kw),
                       reads, writes, inc=inc)

    def act(self, out, in_, func, reads, writes, **kw):
        nc = self.nc
        return self.op("act", lambda: nc.scalar.activation(out=out, in_=in_, func=func, **kw), reads, writes)

    def tt(self, out, in0, in1, op, reads, writes, eng="dve"):
        e = self.eng[eng]
        return self.op(eng, lambda: e.tensor_tensor(out, in0, in1, op), reads, writes)

    def ts(self, out, in0, s1, s2, op0, op1, reads, writes, eng="dve"):
        e = self.eng[eng]
        if op1 is None:
            return self.op(eng, lambda: e.tensor_scalar(out, in0, s1, s2, op0), reads, writes)
        return self.op(eng, lambda: e.tensor_scalar(out, in0, s1, s2, op0, op1), reads, writes)

    def stt(self, out, in0, scalar, in1, op0, op1, reads, writes, eng="dve"):
        e = self.eng[eng]
        return self.op(eng, lambda: e.scalar_tensor_tensor(out, in0, scalar, in1, op0, op1), reads, writes)

    def cp(self, out, in_, reads, writes, eng="dve"):
        e = self.eng[eng]
        return self.op(eng, lambda: e.tensor_copy(out, in_), reads, writes)

    def memset(self, ap, val, writes, eng="dve"):
        e = self.eng[eng]
        return self.op(eng, lambda: e.memset(ap, val), (), writes)


class Ring:
    def __init__(self, items):
        self.items = list(items)
        self.i = 0

    def next(self):
        it = self.items[self.i % len(self.items)]
        self.i += 1
        return it


class StopBuild(Exception):
    pass


DBG = {"att_iters": None, "stage": None}


class Stream:
    def __init__(self, banks):
        self.banks = list(banks)
        self.ring = list(banks)
        self.ptr = 0

    def nbank(self, reserve=False):
        b = self.ring[self.ptr % len(self.ring)]
        if reserve:
            self.ring.remove(b)
        else:
            self.ptr += 1
        return b

    def release(self, b):
        self.ring.append(b)
        self.ring.sort()


def build_program(L):
    nc = bass.Bass("TRN2", target_bir_lowering=False)
    es = ExitStack()
    B = Builder(nc, es, L)
    E = es.enter_context

    def din(name, shape):
        return nc.dram_tensor(name, list(shape), F32, kind="ExternalInput").ap()

    x_d = din("x", [S, D])
    out_d = nc.dram_tensor("out", [S, D], F32, kind="ExternalOutput").ap()
    c_ident = din("c_ident", [128, 128])
    c_tri = din("c_tri", [128, 4, 128])
    c_ntri = din("c_ntri", [128, 2, 128])
    c_pool = din("c_pool", [128, 4, 3, 128])
    c_invcnt = din("c_invcnt", [1, 4 * 128])
    W = []
    for l in range(L):
        W.append(dict(
            w_in=din(f"w_in{l}", [D, INW]),
            w_branch=din(f"w_branch{l}", [3, 512, D]),
            w_out=din(f"w_out{l}", [D, D]),
            w_up=din(f"w_up{l}", [D, 2 * DFF]),
            w_down=din(f"w_down{l}", [DFF, D]),
            gcols=din(f"gcols{l}", [128, 2 * KC]),
            gpost=din(f"gpost{l}", [1, 2 * D]),
            lngb=din(f"lngb{l}", [1, 2 * 512]),
            sgwT=din(f"sgwT{l}", [128, 4, 128]),
            sgb=din(f"sgb{l}", [1, 512]),
            poolw=din(f"poolw{l}", [128, 4, 128]),
            pscale=din(f"pscale{l}", [128, 4]),
            convp=din(f"convp{l}", [128, 2 * NFC, 4]),
        ))

    def sb(name, shape, dt):
        return E(nc.sbuf_tensor("sb_" + name, list(shape), dt))

    NX = 3
    xts = [sb(f"xt{i}", [128, NST, D], F32) for i in range(NX)]
    xs_res = [[Res(f"x{i}_{st}") for st in range(NST)] for i in range(NX)]
    kscr = nc.dram_tensor("kscr", [L, NT, 4, 128, TT], BF16).ap()
    kscr_res = [[[Res(f"kscr{l}_{t}_{p}") for p in range(4)] for t in range(NT)] for l in range(L)]
    vscr = nc.dram_tensor("vscr", [L, NT, 4, 128, NST * 128], BF16).ap()
    vscr_res = [[[Res(f"vscr{l}_{t}_{p}") for p in range(4)] for t in range(NT)] for l in range(L)]
    Vtile = sb("Vtile", [128, NST, 512], BF16)
    Vt_res = [Res(f"Vt{st}") for st in range(NST)]
    NVB = 3
    vbuf = [sb(f"vbuf{i}", [128, NST, 128], BF16) for i in range(NVB)]
    vbuf_res = [Res(f"vbuf{i}") for i in range(NVB)]
    vrr = [0]
    kbuf = [sb(f"kbuf{i}", [128, TT], BF16) for i in range(NVB)]
    kbuf_res = [Res(f"kbuf{i}") for i in range(NVB)]
    krr = [0]

    identF = sb("identF", [128, 128], F32)
    tri = sb("tri", [128, 4, 128], BF16)
    ntri = sb("ntri", [128, 2, 128], BF16)
    leF = sb("leF", [128, 128], F32)
    poolM = sb("poolM", [128, 4, 3, 128], BF16)
    invcnt = sb("invcnt", [128, 4 * 128], F32)
    ones_row = sb("ones_row", [1, 128], BF16)
    zerosb = sb("zerosb", [128, 64], BF16)
    const_res = Res("consts")

    P_ = []
    for l in range(L):
        P_.append(dict(
            gcols=sb(f"gcols{l}", [128, 2 * KC], F32),
            sgwT=sb(f"sgwT{l}", [128, 4, 128], BF16),
            sgbhi=sb(f"sgbhi{l}", [1, 512], BF16),
            sgblo=sb(f"sgblo{l}", [1, 512], BF16),
            poolw=sb(f"poolw{l}", [128, 4, 128], BF16),
            pscale=sb(f"pscale{l}", [128, 4], F32),
            convp=sb(f"convp{l}", [128, 2 * NFC, 4], F32),
            halo=sb(f"halo{l}", [128, 2 * NFC, 2], F32),
            xp_prev=sb(f"xp_prev{l}", [128, 512], BF16),
            res=Res(f"params{l}"),
            halo_res=[Res(f"halo{l}_{c}") for c in range(2 * NFC)],
            xpp_res=Res(f"xpp{l}"),
        ))

    gpost_sb = sb("gpost_sb", [128, D], F32)
    gpost_res = Res("gpost")
    lngb_sb = sb("lngb_sb", [128, 2 * 512], F32)
    lngb_res = Res("lngb")
    NW = 3
    wbuf = [sb(f"wbuf{i}", [128, 4096], BF16) for i in range(NW)]
    wres = [Res(f"wbuf{i}") for i in range(NW)]
    wrr = [0]

    NS16 = 73
    NS32 = 10
    s16 = sb("s16", [128, NS16, 512], BF16)
    s32 = sb("s32", [128, NS32, 512], F32)
    s16_res = [Res(f"s16_{i}") for i in range(NS16)]
    s32_res = [Res(f"s32_{i}") for i in range(NS32)]
    free16 = list(range(NS16))
    free32 = list(range(NS32))
    peak = {"16": 0, "32": 0}

    class Slot:
        __slots__ = ("i", "ap", "res", "kind")

        def __init__(self, i, ap, res, kind):
            self.i, self.ap, self.res, self.kind = i, ap, res, kind

    def a16():
        assert free16, "out of bf16 slots"
        i = free16.pop(0)
        peak["16"] = max(peak["16"], NS16 - len(free16))
        return Slot(i, s16[:, i, :], s16_res[i], 16)

    def a32():
        assert free32, "out of fp32 slots"
        i = free32.pop(0)
        peak["32"] = max(peak["32"], NS32 - len(free32))
        return Slot(i, s32[:, i, :], s32_res[i], 32)

    def fr(*slots):
        for s_ in slots:
            (free16 if s_.kind == 16 else free32).append(s_.i)

    junk = sb("junk", [128, 1024], BF16)
    junk_res = Res("junk")
    stat = sb("stat", [128, 16, 8], F32)
    stat_res = [Res(f"stat{i}") for i in range(16)]
    stat_ring = Ring(range(16))

    ps = E(nc.psum_tensor("ps", [128, 8, 512], F32))
    bank_res = [Res(f"bank{i}") for i in range(8)]
    SM = Stream(range(0, 5))
    SF = Stream(range(5, 8))
    SM.dg = sb("dg_m", [128, NST, 128], F32)
    SM.dg_res = [Res(f"dgm{i}") for i in range(NST)]
    SF.dg, SF.dg_res = SM.dg, SM.dg_res

    def sdma(q, out_ap, in_ap):
        B.dma(q, out_ap, in_ap, [], [Res("setup")], "c_" + q)

    sdma("sp", identF[:], c_ident)
    le_res = Res("le")
    B.dma("sp", leF[:], c_tri[:, 3, :], [], [le_res], "cle")
    sdma("sp", invcnt[:], c_invcnt.partition_broadcast(128))
    sdma("pool", tri[:], c_tri)
    sdma("pool", ntri[:], c_ntri)
    sdma("pool", poolM[:], c_pool)
    B.memset(ones_row[:], 1.0, [Res("setup")])
    B.memset(zerosb[:], 0.0, [Res("setup")])
    for l in range(L):
        p = P_[l]
        w = W[l]
        sdma("sp", p["gcols"][:], w["gcols"])
        sdma("sp", p["pscale"][:], w["pscale"])
        sdma("sp", p["convp"][:], w["convp"])
        sdma("pool", p["poolw"][:], w["poolw"])
        tmp = a32()
        B.dma("sp", tmp.ap, w["sgwT"].rearrange("p g t -> p (g t)"), [], [tmp.res], f"ctmp{l}")
        for g in range(4):
            B.tt(p["sgwT"][:, g, :], tmp.ap[:, g * 128:(g + 1) * 128], leF[:], ALU.mult,
                 [tmp.res, le_res], [Res("setup")])
        fr(tmp)
        sgb_res = Res("sgb")
        t32a, t32b = a32(), a32()
        B.dma("sp", t32a.ap[0:1, :], w["sgb"], [], [sgb_res], f"csgb{l}")
        B.cp(p["sgbhi"][:], t32a.ap[0:1, :], [sgb_res], [sgb_res])
        B.cp(t32b.ap[0:1, :], p["sgbhi"][:], [sgb_res], [sgb_res])
        B.tt(p["sgblo"][:], t32a.ap[0:1, :], t32b.ap[0:1, :], ALU.subtract, [sgb_res], [sgb_res])
        fr(t32a, t32b)
        B.memset(p["halo"][:], 0.0, [Res("setup")])
        B.memset(p["xp_prev"][:], 0.0, [Res("setup")])

    for e in ("pe", "act", "dve", "pool", "sp"):
        for k in list(B.dcount):
            B.eng[e].wait_ge(B.sems[k], B.dcount[k])
            B.waited[e][k] = B.dcount[k]
        if B.seq["dve"] > 0:
            B.eng[e].wait_ge(B.sems["dve"], B.seq["dve"])
            B.waited[e]["dve"] = B.seq["dve"]
    for r_ in s32_res + s16_res:
        r_.w = None
        r_.r = {}

    def wload(src3, nk, ncols):
        i = wrr[0] % NW
        wrr[0] += 1
        dst = wbuf[i][:, 0:nk * ncols].rearrange("p (k c) -> p k c", k=nk)
        B.dma("pool", dst, src3, [], [wres[i]], f"w{i}")
        return dst, wres[i]

    def win_cols(l, c0, n):
        return W[l]["w_in"][:, c0:c0 + n].rearrange("(k p) c -> p k c", p=128)

    def rstd_chain(si, src_col, n):
        r = stat_res[si]
        B.act(stat[:, si, 6:7], stat[:, si, src_col:src_col + 1], AF.Ln, [r], [r], scale=1.0 / n, bias=EPS)
        B.act(stat[:, si, 7:8], stat[:, si, 6:7], AF.Exp, [r], [r], scale=-0.5)

    def prenorm_gen(l, which, xb, S_):
        p = P_[l]
        xt = xts[xb]
        x_res = xs_res[xb]
        hT = [a16() for _ in range(KC)]
        for st in range(NST):
            si = stat_ring.next()
            r = stat_res[si]
            B.memset(stat[:, si, 0:1], 0.0, [r])
            B.act(junk[:, :], xt[:, st, :], AF.Square,
                  [x_res[st], r], [junk_res, r], accum_out=stat[:, si, 0:1])
            rstd_chain(si, 0, D)
            B.ts(S_.dg[:, st, :], identF[:], stat[:, si, 7:8], None, ALU.mult, None,
                 [r], [S_.dg_res[st]])
        yield hT
        pend = None
        for kc in range(KC + 1):
            if kc < KC:
                b = S_.nbank()
                for st in range(NST):
                    B.mm(ps[:, b, st * 128:(st + 1) * 128], xt[:, st, kc * 128:(kc + 1) * 128], S_.dg[:, st, :],
                         [x_res[st], S_.dg_res[st]], [bank_res[b]], inc=(st == NST - 1))
            if pend is not None:
                pk, pb_ = pend
                B.act(hT[pk].ap, ps[:, pb_, :], AF.Identity, [bank_res[pb_]], [hT[pk].res],
                      scale=p["gcols"][:, which * KC + pk: which * KC + pk + 1])
            pend = (kc, b) if kc < KC else None
            yield hT

    def proj_fm(bank, wt, wr, k0, hT):
        for kc in range(KC):
            B.mm(ps[:, bank, :], wt[:, kc, k0:k0 + 128], hT[kc].ap, [wr, hT[kc].res], [bank_res[bank]],
                 start=(kc == 0), stop=(kc == KC - 1), inc=(kc == KC - 1))

    def proj_tm(bank, wt, wr, st, hT):
        for kc in range(KC):
            B.mm(ps[:, bank, :], hT[kc].ap[:, st * 128:(st + 1) * 128], wt[:, kc, :], [wr, hT[kc].res],
                 [bank_res[bank]], start=(kc == 0), stop=(kc == KC - 1), inc=(kc == KC - 1))

    def postnorm_residual(xb, st, A, bank):
        xt = xts[xb]
        xr = xs_res[xb][st]
        si = stat_ring.next()
        r = stat_res[si]
        br = bank_res[bank]
        B.memset(stat[:, si, 0:2], 0.0, [r])
        B.act(junk[:, 0:512], A.ap, AF.Square, [A.res, r], [junk_res, r], accum_out=stat[:, si, 0:1])
        B.act(junk[:, 512:1024], ps[:, bank, :], AF.Square, [br, r], [junk_res, r], accum_out=stat[:, si, 1:2])
        B.tt(stat[:, si, 2:3], stat[:, si, 0:1], stat[:, si, 1:2], ALU.add, [r], [r])
        rstd_chain(si, 2, D)
        pb = a32()
        B.stt(A.ap, A.ap, stat[:, si, 7:8], gpost_sb[:, 0:512],
              ALU.mult, ALU.mult, [A.res, r, gpost_res], [A.res])
        B.stt(pb.ap, ps[:, bank, :], stat[:, si, 7:8], gpost_sb[:, 512:1024],
              ALU.mult, ALU.mult, [br, r, gpost_res], [pb.res])
        B.tt(xt[:, st, 0:512], xt[:, st, 0:512], A.ap, ALU.add, [xr, A.res], [xr])
        B.tt(xt[:, st, 512:1024], xt[:, st, 512:1024], pb.ap, ALU.add, [xr, pb.res], [xr])
        fr(pb)

    def attention_gen(l, T, pr, qs, Kt, aT_slot, S_):
        accb = S_.nbank(reserve=True)
        ar = bank_res[accb]
        Eslots = [a32() for _ in range(2)]
        Pslots = [a16() for _ in range(4)]
        wslots = [a16() for _ in range(3)]
        psum_slots = [[a16(), a16()] for _ in range(2)]
        Ering, Pring, wring = Ring(Eslots), Ring(Pslots), Ring(wslots)
        for h in range(2):
            B.mm(ps[h * 64:(h + 1) * 64, accb, :], zerosb[:, :], qs[0].ap,
                 [qs[0].res], [ar], start=True, stop=False, inc=(h == 1), tile_position=(0, h * 64),
                 skip_group_check=True)
        for hf in range(2):
            B.memset(psum_slots[hf][0].ap, 0.0, [psum_slots[hf][0].res])
        blocks = list(range(T * 4 + 3, -1, -1))
        items = []
        for j in blocks:
            jl = j - T * 4
            c0 = jl * 128 if jl >= 0 else 0
            for hf in range(2):
                if c0 >= (hf + 1) * 256:
                    continue
                lc0 = max(0, c0 - hf * 256)
                diag = (jl >= 0 and c0 // 256 == hf)
                items.append((j, hf, lc0, diag))
        firstj = {1: T * 4 + 3, 0: T * 4 + 1}
        n = len(items)
        st_ = {}
        pidx = [0, 0]
        vchunk = {}
        kchunk = {}

        def ensure_v(tp):
            if tp >= 0 and tp not in vchunk:
                k = vrr[0] % NVB
                vrr[0] += 1
                B.dma("sp", vbuf[k][:], vscr[l, tp, pr].rearrange("s (b c) -> s b c", b=NST),
                      [vscr_res[l][tp][pr]], [vbuf_res[k]], f"vld{k}")
                vchunk[tp] = k

        def ensure_k(tp):
            if tp >= 0 and tp not in kchunk:
                k = krr[0] % NVB
                krr[0] += 1
                B.dma("sp", kbuf[k][:], kscr[l, tp, pr], [kscr_res[l][tp][pr]], [kbuf_res[k]], f"kld{k}")
                kchunk[tp] = k

        def vblock(j, h):
            jl = j - T * 4
            if jl >= 0:
                return Vtile[:, jl, (2 * pr + h) * 64:(2 * pr + h + 1) * 64], Vt_res[jl]
            tp = j // 4
            ensure_v(tp)
            ensure_v(tp - 1)
            k = vchunk[tp]
            return vbuf[k][:, j % 4, h * 64:(h + 1) * 64], vbuf_res[k]

        def kblock(j, rows):
            jl = j - T * 4
            if jl >= 0:
                return Kt.ap[rows, jl * 128:(jl + 1) * 128], Kt.res
            tp = j // 4
            ensure_k(tp)
            ensure_k(tp - 1)
            k = kchunk[tp]
            return kbuf[k][rows, (j % 4) * 128:(j % 4 + 1) * 128], kbuf_res[k]

        ensure_k(T - 1)
        ensure_v(T - 1)

        def view(ap2d, lc0):
            if lc0 == 0:
                return ap2d
            return ap2d.rearrange("p (h c) -> p h c", h=2)[:, :, lc0:]

        def stageA(i):
            j, hf, lc0, diag = items[i]
            zb = S_.nbank()
            kap, kres = kblock(j, slice(0, 128))
            for h in range(2):
                B.mm(ps[:, zb, h * 256 + lc0:(h + 1) * 256], kap, qs[h].ap[:, hf * 256 + lc0:(hf + 1) * 256],
                     [kres, qs[h].res], [bank_res[zb]], start=(h == 0), stop=(h == 1), inc=(h == 1))
            if DBG["stage"] == -1:
                raise StopBuild()
            e_ = Ering.next()
            B.act(view(e_.ap, lc0), view(ps[:, zb, :], lc0), AF.Exp, [bank_res[zb]], [e_.res])
            if DBG["stage"] == -2:
                raise StopBuild()
            P = Pring.next()
            B.act(view(P.ap, lc0), view(e_.ap, lc0), AF.Ln, [e_.res], [P.res], bias=1.0)
            if DBG["stage"] == -3:
                raise StopBuild()
            if diag:
                for h in range(2):
                    cs = slice(h * 256 + lc0, h * 256 + lc0 + 128)
                    B.tt(P.ap[:, cs], P.ap[:, cs], tri[:, 2, :], ALU.mult, [P.res], [P.res])
            st_[i] = dict(P=P, zb=zb)

        def stageB(i):
            j, hf, lc0, diag = items[i]
            d = st_[i]
            P, b2 = d["P"], d["zb"]
            first = (j == firstj[hf])
            last = (j == 0)
            cur = psum_slots[hf][pidx[hf] % 2]
            if lc0 == 0:
                segs = [slice(0, 512)]
            else:
                segs = [slice(h * 256 + lc0, (h + 1) * 256) for h in range(2)]
            for si_, sg_ in enumerate(segs):
                lastseg = (si_ == len(segs) - 1)
                B.mm(ps[:, b2, sg_], ntri[:, 0, :], P.ap[:, sg_], [P.res], [bank_res[b2]],
                     start=False, stop=first, inc=(first and lastseg), skip_group_check=True)
                if not first:
                    B.mm(ps[:, b2, sg_], ntri[:, 1, :], cur.ap[:, sg_], [cur.res], [bank_res[b2]],
                         start=False, stop=True, inc=lastseg, skip_group_check=True)
            w_ = wring.next()
            B.act(view(w_.ap, lc0), view(ps[:, b2, :], lc0), AF.Exp, [bank_res[b2]], [w_.res])
            if diag:
                for h in range(2):
                    cs = slice(h * 256 + lc0, h * 256 + lc0 + 128)
                    B.tt(w_.ap[:, cs], w_.ap[:, cs], tri[:, 2, :], ALU.mult, [w_.res], [w_.res])
            if not last:
                nxt = psum_slots[hf][(pidx[hf] + 1) % 2]
                if lc0 > 0:
                    v0 = lambda ap: ap.rearrange("p (h c) -> p h c", h=2)[:, :, 0:lc0]
                    B.cp(v0(nxt.ap), v0(cur.ap), [cur.res], [nxt.res])
                B.tt(view(nxt.ap, lc0), view(cur.ap, lc0), view(P.ap, lc0), ALU.add, [cur.res, P.res], [nxt.res])
                pidx[hf] += 1
            d["w"] = w_

        def stageC(i):
            j, hf, lc0, diag = items[i]
            d = st_.pop(i)
            w_ = d["w"]
            for h in range(2):
                vap, vres = vblock(j, h)
                B.mm(ps[h * 64:(h + 1) * 64, accb, hf * 256 + lc0:(hf + 1) * 256], vap,
                     w_.ap[:, h * 256 + lc0:(h + 1) * 256], [vres, w_.res], [ar], start=False, stop=(j == 0),
                     inc=(h == 1), tile_position=(0, h * 64), skip_group_check=True)

        SK = 2
        if DBG["att_iters"] is not None:
            ii_ = DBG["att_iters"]
            stageA(ii_)
            if DBG["stage"] >= 1:
                stageB(ii_)
            if DBG["stage"] >= 2:
                stageC(ii_)
            raise StopBuild()
        for i in range(n + 2 * SK):
            if i < n:
                stageA(i)
            if 0 <= i - SK < n:
                stageB(i - SK)
            if 0 <= i - 2 * SK < n:
                stageC(i - 2 * SK)
            if i < n + 2 * SK - 1:
                yield "att"
        B.act(aT_slot.ap, ps[:, accb, :], AF.Copy, [ar], [aT_slot.res])
        S_.release(accb)
        fr(*Eslots, *Pslots, *wslots, *psum_slots[0], *psum_slots[1])

    def n_att_steps(T):
        return 4 * (8 * T + 6 + 4)

    def mixer_gen(l, T):
        S_ = SM
        p = P_[l]
        xb = T % NX
        xt = xts[xb]
        if l == 0:
            xin = x_d[T * TT:(T + 1) * TT, :].rearrange("(s p) d -> p s d", p=128)
            B.dma("sp", xt[:], xin, [], xs_res[xb], "xin")
        hT = None
        for hT in prenorm_gen(l, 0, xb, S_):
            yield "pre"
        wt, wr = wload(win_cols(l, 1024, 512), KC, 512)
        for st in range(NST):
            b = S_.nbank()
            proj_tm(b, wt, wr, st, hT)
            B.act(Vtile[:, st, :], ps[:, b, :], AF.Copy, [bank_res[b]], [Vt_res[st]])
            yield "pre"
        if T < NT - 1:
            for pr in range(4):
                B.dma("sp", vscr[l, T, pr].rearrange("s (b c) -> s b c", b=NST), Vtile[:, :, pr * 128:(pr + 1) * 128],
                      Vt_res, [vscr_res[l][T][pr]], f"vst{pr}")
        B.dma("sp", lngb_sb[:], W[l]["lngb"].partition_broadcast(128), [], [lngb_res], "lngb")
        wt, wr = wload(win_cols(l, 2048, 512), KC, 512)
        vsg = [a16() for _ in range(NST)]
        for st in range(NST):
            b = S_.nbank()
            proj_tm(b, wt, wr, st, hT)
            si = stat_ring.next()
            r = stat_res[si]
            gl = a32()
            B.memset(stat[:, si, 0:2], 0.0, [r])
            B.act(gl.ap, ps[:, b, :], AF.Gelu_apprx_tanh, [bank_res[b], r], [gl.res, r], accum_out=stat[:, si, 0:1])
            B.act(junk[:, 0:512], gl.ap, AF.Square, [gl.res, r], [junk_res, r], accum_out=stat[:, si, 1:2])
            B.ts(stat[:, si, 3:4], stat[:, si, 0:1], 1.0 / 512, None, ALU.mult, None, [r], [r])
            B.tt(stat[:, si, 4:5], stat[:, si, 3:4], stat[:, si, 3:4], ALU.mult, [r], [r])
            B.stt(stat[:, si, 5:6], stat[:, si, 1:2], 1.0 / 512, stat[:, si, 4:5], ALU.mult, ALU.subtract, [r], [r])
            B.act(stat[:, si, 6:7], stat[:, si, 5:6], AF.Ln, [r], [r], bias=EPS)
            B.act(stat[:, si, 7:8], stat[:, si, 6:7], AF.Exp, [r], [r], scale=-0.5)
            B.ts(gl.ap, gl.ap, stat[:, si, 3:4], stat[:, si, 7:8], ALU.subtract, ALU.mult, [gl.res, r], [gl.res])
            B.tt(gl.ap, gl.ap, lngb_sb[:, 0:512], ALU.mult, [gl.res, lngb_res], [gl.res])
            B.tt(vsg[st].ap, gl.ap, lngb_sb[:, 512:1024], ALU.add, [gl.res, lngb_res], [vsg[st].res])
            fr(gl)
            yield "pre"
        wt, wr = wload(win_cols(l, 2560, 512), KC, 512)
        xp = [a16() for _ in range(NST)]
        for st in range(NST):
            b = S_.nbank()
            proj_tm(b, wt, wr, st, hT)
            B.act(xp[st].ap, ps[:, b, :], AF.Copy, [bank_res[b]], [xp[st].res])
            yield "pre"
        wt, wr = wload(win_cols(l, 1536, 512), KC, 512)
        bT = [a16() for _ in range(4)]
        for g in range(4):
            b = S_.nbank()
            proj_fm(b, wt, wr, g * 128, hT)
            u = a16()
            B.act(u.ap, ps[:, b, :], AF.Gelu_apprx_tanh, [bank_res[b]], [u.res])
            b2 = S_.nbank()
            for st in range(NST):
                cs = slice(st * 128, (st + 1) * 128)
                B.mm(ps[:, b2, cs], vsg[st].ap[:, g * 128:(g + 1) * 128], p["sgwT"][:, g, :],
                     [vsg[st].res], [bank_res[b2]], start=True, stop=False)
                B.mm(ps[:, b2, cs], ones_row[0:1, :], p["sgbhi"][0:1, g * 128:(g + 1) * 128],
                     [], [bank_res[b2]], start=False, stop=False)
                B.mm(ps[:, b2, cs], ones_row[0:1, :], p["sgblo"][0:1, g * 128:(g + 1) * 128],
                     [], [bank_res[b2]], start=False, stop=True, inc=(st == NST - 1))
            B.tt(bT[g].ap, ps[:, b2, :], u.ap, ALU.mult, [bank_res[b2], u.res], [bT[g].res])
            fr(u)
            yield "pre"
        fr(*vsg)
        cT = [a16() for _ in range(4)]
        for g in range(4):
            b = S_.nbank()
            for st in range(NST):
                cs = slice(st * 128, (st + 1) * 128)
                gs = slice(g * 128, (g + 1) * 128)
                if T == 0 and st == 0:
                    B.mm(ps[:, b, cs], xp[0].ap[:, gs], poolM[:, g, 2, :], [xp[0].res], [bank_res[b]],
                         start=True, stop=True)
                else:
                    B.mm(ps[:, b, cs], xp[st].ap[:, gs], poolM[:, g, 0, :], [xp[st].res],
                         [bank_res[b]], start=True, stop=False)
                    if st == 0:
                        pv, pvr = p["xp_prev"][:, gs], p["xpp_res"]
                    else:
                        pv, pvr = xp[st - 1].ap[:, gs], xp[st - 1].res
                    B.mm(ps[:, b, cs], pv, poolM[:, g, 1, :], [pvr], [bank_res[b]],
                         start=False, stop=True, inc=(st == NST - 1))
            pT = a16()
            if T == 0:
                B.tt(pT.ap[:, 0:128], ps[:, b, 0:128], invcnt[:, g * 128:(g + 1) * 128], ALU.mult,
                     [bank_res[b]], [pT.res])
                B.act(pT.ap[:, 128:512], ps[:, b, 128:512], AF.Identity, [bank_res[b]], [pT.res], scale=1.0 / WINS[g])
            else:
                B.act(pT.ap, ps[:, b, :], AF.Identity, [bank_res[b]], [pT.res], scale=1.0 / WINS[g])
            b2 = S_.nbank()
            B.mm(ps[:, b2, :], p["poolw"][:, g, :], pT.ap, [pT.res], [bank_res[b2]], inc=True)
            B.act(cT[g].ap, ps[:, b2, :], AF.Identity, [bank_res[b2]], [cT[g].res], scale=p["pscale"][:, g:g + 1])
            fr(pT)
            yield "pre"
        B.cp(p["xp_prev"][:, :], xp[3].ap, [xp[3].res], [p["xpp_res"]])
        fr(*xp)
        wq, wqr = wload(win_cols(l, 0, 512), KC, 512)
        wk, wkr = wload(win_cols(l, 512, 512), KC, 512)
        qs = [(a16(), a16()) for _ in range(4)]
        Kt = [a16() for _ in range(4)]
        for pr in range(4):
            bq = S_.nbank()
            proj_fm(bq, wq, wqr, pr * 128, hT)
            bk = S_.nbank()
            proj_fm(bk, wk, wkr, pr * 128, hT)
            for h in range(2):
                rows, orow = slice(h * 64, (h + 1) * 64), slice((1 - h) * 64, (2 - h) * 64)
                B.act(qs[pr][h].ap[rows, :], ps[rows, bq, :], AF.Identity, [bank_res[bq]], [qs[pr][h].res], scale=0.125)
                B.memset(qs[pr][h].ap[orow, :], 0.0, [qs[pr][h].res])
            B.cp(Kt[pr].ap, ps[:, bk, :], [bank_res[bk]], [Kt[pr].res])
            if T < NT - 1:
                B.dma("sp", kscr[l, T, pr], Kt[pr].ap, [Kt[pr].res], [kscr_res[l][T][pr]], f"kst{pr}")
            yield "pre"
        aT = [a16() for _ in range(4)]
        for pr in range(4):
            yield from attention_gen(l, T, pr, qs[pr], Kt[pr], aT[pr], S_)
            fr(*qs[pr], Kt[pr])
        yield "attdone"
        macc = [a32() for _ in range(KC)]
        merged = [a16() for _ in range(KC)]
        srcs = [aT, bT, cT]
        for i in range(3):
            wb, wbr = wload(W[l]["w_branch"][i].rearrange("(r p) c -> p r c", p=128), 4, D)
            for half in range(2):
                wg, wgr = wload(win_cols(l, 3072 + i * D + half * 512, 512), KC, 512)
                for dc in range(4):
                    d = half * 4 + dc
                    by = S_.nbank()
                    for r_ in range(4):
                        B.mm(ps[:, by, :], wb[:, r_, d * 128:(d + 1) * 128], srcs[i][r_].ap, [wbr, srcs[i][r_].res],
                             [bank_res[by]], start=(r_ == 0), stop=(r_ == 3), inc=(r_ == 3))
                    bg = S_.nbank()
                    proj_fm(bg, wg, wgr, dc * 128, hT)
                    sg = a32()
                    B.act(sg.ap, ps[:, bg, :], AF.Sigmoid, [bank_res[bg]], [sg.res])
                    if i == 0:
                        B.tt(macc[d].ap, ps[:, by, :], sg.ap, ALU.mult, [bank_res[by], sg.res], [macc[d].res])
                    else:
                        B.tt(sg.ap, ps[:, by, :], sg.ap, ALU.mult, [bank_res[by], sg.res], [sg.res])
                        if i == 1:
                            B.tt(macc[d].ap, macc[d].ap, sg.ap, ALU.add, [macc[d].res, sg.res], [macc[d].res])
                        else:
                            B.tt(merged[d].ap, macc[d].ap, sg.ap, ALU.add, [macc[d].res, sg.res], [merged[d].res])
                    fr(sg)
                    yield "post"
        fr(*aT, *bT, *cT, *hT, *macc)
        A = [a32() for _ in range(NST)]
        for half in range(2):
            wo, wor = wload(W[l]["w_out"][:, half * 512:(half + 1) * 512].rearrange("(k p) c -> p k c", p=128), KC, 512)
            for st in range(NST):
                b = S_.banks[st]
                for kc in range(KC):
                    B.mm(ps[:, b, :], merged[kc].ap[:, st * 128:(st + 1) * 128], wo[:, kc, :],
                         [merged[kc].res, wor], [bank_res[b]], start=(kc == 0), stop=(kc == KC - 1),
                         inc=(kc == KC - 1))
                if half == 0:
                    B.act(A[st].ap, ps[:, b, :], AF.Copy, [bank_res[b]], [A[st].res])
                yield "post"
        B.dma("sp", gpost_sb[:], W[l]["gpost"][:, 0:D].partition_broadcast(128), [], [gpost_res], "gpost")
        for st in range(NST):
            postnorm_residual(xb, st, A[st], S_.banks[st])
            yield "post"
        fr(*merged, *A)

    def ffn_gen(l, T):
        S_ = SF
        p = P_[l]
        xb = T % NX
        hT = None
        for hT in prenorm_gen(l, 1, xb, S_):
            yield "ffn"
        gsl = [a16() for _ in range(NFC)]
        cw = p["convp"]
        NH = 2 * NFC
        hb = {}
        t1s = {}
        wts = {}

        def ch_of(h):
            i = h // 2
            return i if h % 2 == 0 else NFC + i

        def get_w(h):
            i = h // 2
            g0 = (i // 4) * 4
            key = (g0, h % 2)
            if key not in wts:
                ng = min(4, NFC - g0)
                c0 = g0 * 128 + (DFF if h % 2 else 0)
                wts[key] = wload(W[l]["w_up"][:, c0:c0 + ng * 128].rearrange("(k p) c -> p k c", p=128), KC, ng * 128)
            wt, wr = wts[key]
            return wt, wr, (i - g0) * 128

        order, gelu_at, prod_at = [], {}, {}
        for g0 in range(0, NFC, 4):
            ids = list(range(g0, min(g0 + 4, NFC)))
            order += [2 * i for i in ids] + [2 * i + 1 for i in ids]
        pos = {h: k for k, h in enumerate(order)}
        for g0 in range(0, NFC, 4):
            ids = list(range(g0, min(g0 + 4, NFC)))
            gelu_at[pos[2 * ids[-1]] + 3] = ids
            for i in ids:
                prod_at.setdefault(pos[2 * i + 1] + 3, []).append(i)

        for step_ in range(NH + 6):
            if step_ < NH:
                h = order[step_]
                wt, wr, k0 = get_w(h)
                b = S_.nbank()
                proj_fm(b, wt, wr, k0, hT)
                hb[h] = b
            if 0 <= step_ - 1 < NH:
                h = order[step_ - 1]
                b = hb[h]
                ch = ch_of(h)
                t1 = a32()
                B.act(t1.ap, ps[:, b, :], AF.Identity, [bank_res[b]], [t1.res],
                      scale=cw[:, ch, 2:3], bias=cw[:, ch, 3:4])
                t1s[h] = t1
            if 0 <= step_ - 2 < NH:
                h = order[step_ - 2]
                b = hb.pop(h)
                ch = ch_of(h)
                hr = p["halo_res"][ch]
                t1 = t1s[h]
                B.stt(t1.ap[:, 1:512], ps[:, b, 0:511], cw[:, ch, 1:2], t1.ap[:, 1:512], ALU.mult, ALU.add,
                      [bank_res[b], t1.res], [t1.res])
                B.stt(t1.ap[:, 2:512], ps[:, b, 0:510], cw[:, ch, 0:1], t1.ap[:, 2:512], ALU.mult, ALU.add,
                      [bank_res[b], t1.res], [t1.res])
                if T > 0:
                    B.stt(t1.ap[:, 0:1], p["halo"][:, ch, 1:2], cw[:, ch, 1:2], t1.ap[:, 0:1], ALU.mult, ALU.add,
                          [hr, t1.res], [t1.res])
                    B.stt(t1.ap[:, 0:2], p["halo"][:, ch, 0:2], cw[:, ch, 0:1], t1.ap[:, 0:2], ALU.mult, ALU.add,
                          [hr, t1.res], [t1.res])
                B.cp(p["halo"][:, ch, :], ps[:, b, 510:512], [bank_res[b]], [hr])
            for i_ in gelu_at.get(step_, ()):
                tg = t1s[2 * i_]
                B.act(tg.ap, tg.ap, AF.Gelu_apprx_tanh, [tg.res], [tg.res])
            for i_ in prod_at.get(step_, ()):
                tg, tv = t1s.pop(2 * i_), t1s.pop(2 * i_ + 1)
                B.tt(gsl[i_].ap, tg.ap, tv.ap, ALU.mult, [tg.res, tv.res], [gsl[i_].res])
                fr(tg, tv)
            yield "ffn"
        assert not t1s and not hb
        fr(*hT)
        for sp_ in range(2):
            A = {}
            for half in range(2):
                bks = {2 * sp_: S_.banks[0], 2 * sp_ + 1: S_.banks[1]}
                i0 = 0
                while i0 < NFC:
                    ng = min(8, NFC - i0)
                    wd, wdr = wload(W[l]["w_down"][i0 * 128:(i0 + ng) * 128, half * 512:(half + 1) * 512]
                                    .rearrange("(r p) c -> p r c", p=128), ng, 512)
                    for ii in range(ng):
                        i = i0 + ii
                        for st in (2 * sp_, 2 * sp_ + 1):
                            b = bks[st]
                            lastmm = (st == 2 * sp_ + 1 and (ii % 4 == 3 or ii == ng - 1))
                            B.mm(ps[:, b, :], gsl[i].ap[:, st * 128:(st + 1) * 128], wd[:, ii, :],
                                 [gsl[i].res, wdr], [bank_res[b]], start=(i == 0), stop=(i == NFC - 1), inc=lastmm)
                        if ii % 4 == 3 or ii == ng - 1:
                            yield "ffn"
                    i0 += ng
                if half == 0:
                    for st in (2 * sp_, 2 * sp_ + 1):
                        A[st] = a32()
                        B.cp(A[st].ap, ps[:, bks[st], :], [bank_res[bks[st]]], [A[st].res])
                    yield "ffn"
            if sp_ == 0:
                B.dma("sp", gpost_sb[:], W[l]["gpost"][:, D:2 * D].partition_broadcast(128), [], [gpost_res], "gpost")
            yield "ffn"
            for st in (2 * sp_, 2 * sp_ + 1):
                postnorm_residual(xb, st, A[st], bks[st])
                yield "ffn"
            fr(*A.values())
        fr(*gsl)
        if l == L - 1:
            xout = out_d[T * TT:(T + 1) * TT, :].rearrange("(s p) d -> p s d", p=128)
            B.dma("sp", xout, xts[xb][:], xs_res[xb], [], "xout")

    units = sorted([(T, l) for T in range(NT) for l in range(L)], key=lambda u: (u[0] + 2 * u[1], -u[1]))
    N_FFN_STEPS = 100

    def step(g):
        try:
            return next(g)
        except StopIteration:
            return None

    pending_ffn = None
    for (T, l) in units:
        mg = mixer_gen(l, T)
        tag = step(mg)
        while tag == "pre":
            tag = step(mg)
        natt = n_att_steps(T)
        every = max(1, natt // N_FFN_STEPS) if pending_ffn is not None else 0
        cnt = 0
        while tag == "att":
            cnt += 1
            if pending_ffn is not None and cnt % every == 0:
                if step(pending_ffn) is None:
                    pending_ffn = None
            tag = step(mg)
        while pending_ffn is not None:
            if step(pending_ffn) is None:
                pending_ffn = None
        while tag is not None:
            tag = step(mg)
        pending_ffn = ffn_gen(l, T)
    while pending_ffn is not None:
        if step(pending_ffn) is None:
            pending_ffn = None

    nc.sync.wait_ge(B.sems["xout"], B.dcount["xout"])
    es.close()
    B.peak = peak
    return nc, B


def _consts():
    ident = np.eye(128, dtype=np.float32)
    s = np.arange(128)[:, None]
    t = np.arange(128)[None, :]
    tri = np.zeros((128, 4, 128), np.float32)
    tri[:, 0, :] = (s >= t)
    tri[:, 1, :] = 1.0
    tri[:, 2, :] = (s < t)
    tri[:, 3, :] = (s <= t)
    pool = np.zeros((128, 4, 3, 128), np.float32)
    invc = np.zeros((4, 128), np.float32)
    for g, win in enumerate(WINS):
        cur = ((s <= t) & (s >= t - win + 1)).astype(np.float32) - win * (s == t)
        prev = (s >= 129 + t - win).astype(np.float32)
        cnt = np.minimum(np.arange(128) + 1, win)
        first = ((s <= t) & (s >= t - win + 1)).astype(np.float32) - cnt[None, :] * (s == t)
        pool[:, g, 0, :] = cur
        pool[:, g, 1, :] = prev
        pool[:, g, 2, :] = first
        invc[g] = 1.0 / cnt
    return ident, tri, pool, invc.reshape(1, 512)


def _layer_inputs(inp, l, k):
    f = lambda a: np.ascontiguousarray(a, dtype=np.float32)
    cols = lambda v: v.reshape(-1, 128).T
    gcols = np.concatenate([cols(inp["norm_pre_mix"][l]), cols(inp["norm_pre_ffn"][l])], axis=1)
    gpost = np.concatenate([inp["norm_post_mix"][l], inp["norm_post_ffn"][l]])[None, :]
    lngb = np.concatenate([inp["sg_ln_g"][l], inp["sg_ln_b"][l]])[None, :]
    convp = np.concatenate([inp["conv_w"][l], inp["conv_b"][l][None, :]], axis=0)
    convp = convp.reshape(4, 2 * NFC, 128).transpose(2, 1, 0)
    return {
        f"w_in{k}": f(inp["w_in"][l]), f"w_branch{k}": f(inp["w_branch"][l]), f"w_out{k}": f(inp["w_out"][l]),
        f"w_up{k}": f(inp["w_up"][l]), f"w_down{k}": f(inp["w_down"][l]),
        f"gcols{k}": f(gcols), f"gpost{k}": f(gpost), f"lngb{k}": f(lngb),
        f"sgwT{k}": f(inp["sg_w"][l].transpose(2, 0, 1)), f"sgb{k}": f(inp["sg_b"][l].reshape(1, 512)),
        f"poolw{k}": f(inp["pool_w"][l].transpose(1, 0, 2)), f"pscale{k}": f(cols(inp["pool_scale"][l])),
        f"convp{k}": f(convp),
    }


_PROG = {}


def _get_prog(L):
    if L not in _PROG:
        _PROG[L] = build_program(L)[0]
    return _PROG[L]


FUSED_LAYERS = 2


def kernel(**inputs):
    inp = {k: np.asarray(v) for k, v in inputs.items()}
    depth = inp["w_in"].shape[0]
    ident, tri, pool, invc = _consts()
    x = np.ascontiguousarray(inp["x"], dtype=np.float32)
    Lp = FUSED_LAYERS
    nc = _get_prog(Lp)
    for l0 in range(0, depth, Lp):
        shared = {"c_ident": ident, "c_tri": tri, "c_pool": pool, "c_invcnt": invc,
                  "c_ntri": np.ascontiguousarray(-tri[:, 0:2, :])}
        for k in range(Lp):
            shared.update(_layer_inputs(inp, l0 + k, k))
        in_maps = [dict(shared, x=x[c]) for c in range(NCORES)]
        res = run_bass_kernel_spmd(nc, in_maps, core_ids=list(range(NCORES)))
        x = np.stack([np.asarray(res.results[c]["out"]) for c in range(NCORES)], axis=0).astype(np.float32)
    return x
```
